# Optimizing a Trainium2 kernel written in Bass

```python
import math
import jax
import jax.numpy as jnp
from jax import lax
import numpy as np

D_MODEL = 2048
BATCH = 8
SEQ = 2048
DEPTH = 2
DEC_BATCH = 32
DEC_SEQ = 8
PAST_LEN = 8192
PAGE_SIZE = 128

D_MIX = D_MODEL
D_HEAD_ATT = 64
D_ATT = D_MIX // 4
H_ATT = D_ATT // D_HEAD_ATT
Q_BLOCK = 128
D_SSD = D_MIX // 2
P_SSD = 64
H_SSD = D_SSD // P_SSD
G_SSD = 2
E_SSD = H_SSD // G_SSD
N_SSD = 128
SSD_CONV_W = 4
SSD_CONV_DIM = D_SSD + 2 * G_SSD * N_SSD
SSD_CHUNK = 128
C_CONF = D_MIX - D_ATT - D_SSD
CONF_W = 31
D_FF = 128 * ((8 * D_MODEL // 3 + 127) // 128)
FFN_CONV_W = 3
D_IN = 3 * D_ATT + D_SSD + SSD_CONV_DIM + H_SSD + 2 * C_CONF
EPS = 1e-6

kernel_name = "hybrid_stickbreak_ssd_conformer_step"


def _rmsnorm(x, g):
    xf = x.astype(jnp.float32)
    xf = xf * lax.rsqrt(jnp.mean(xf * xf, axis=-1, keepdims=True) + EPS)
    return xf.astype(x.dtype) * g


def _layernorm(x, g, b):
    xf = x.astype(jnp.float32)
    xc = xf - jnp.mean(xf, axis=-1, keepdims=True)
    xf = xc * lax.rsqrt(jnp.mean(xc * xc, axis=-1, keepdims=True) + EPS)
    return xf.astype(x.dtype) * g + b


def _causal_dwconv(x, prev, w, b):
    k = w.shape[0]
    xpad = jnp.concatenate([prev.astype(x.dtype), x], axis=1)
    y = lax.conv_general_dilated(
        xpad, w[:, None, :].astype(x.dtype), window_strides=(1,), padding="VALID",
        dimension_numbers=("NWC", "WIO", "NWC"), feature_group_count=x.shape[-1])
    return y + b, xpad[:, xpad.shape[1] - (k - 1):]


def _sb_block(q, k, v, bias, q_pos, k_pos):
    z = jnp.einsum("bqhd,bkhd->bhqk", q, k, preferred_element_type=jnp.float32) * (D_HEAD_ATT ** -0.5)
    z = z + bias.astype(jnp.float32)[None, :, None, None]
    earlier = k_pos[None, :] < q_pos[:, None]
    log_keep = jnp.where(earlier, jax.nn.log_sigmoid(-z), 0.0)
    later_sum = lax.cumsum(log_keep, axis=3, reverse=True) - log_keep
    w = jnp.where(earlier, jnp.exp(jax.nn.log_sigmoid(z) + later_sum), 0.0)
    return jnp.einsum("bhqk,bkhd->bqhd", w.astype(v.dtype), v)


def _sb_attention(q, k, v, bias, q_pos, k_pos):
    b, tq, h, d = q.shape
    blk = Q_BLOCK if tq % Q_BLOCK == 0 else tq
    nb = tq // blk
    qb = q.reshape(b, nb, blk, h, d).swapaxes(0, 1)
    pb = q_pos.reshape(nb, blk)
    out = lax.map(lambda qp: _sb_block(qp[0], k, v, bias, qp[1], k_pos), (qb, pb))
    return out.swapaxes(0, 1).reshape(b, tq, h, d)


def _segsum(a):
    t = a.shape[-1]
    cs = jnp.cumsum(a, axis=-1)
    diff = cs[..., :, None] - cs[..., None, :]
    return jnp.where(jnp.tril(jnp.ones((t, t), dtype=bool)), diff, -jnp.inf)


def _ssd(x, dt, a_head, bm, cm, h0):
    b, t = x.shape[:2]
    L = SSD_CHUNK if t % SSD_CHUNK == 0 else t
    c = t // L
    xf = (x.astype(jnp.float32) * dt[..., None]).reshape(b, c, L, G_SSD, E_SSD, P_SSD)
    a = (dt * a_head).reshape(b, c, L, G_SSD, E_SSD).transpose(0, 3, 4, 1, 2)
    bc = bm.astype(jnp.float32).reshape(b, c, L, G_SSD, N_SSD)
    cc = cm.astype(jnp.float32).reshape(b, c, L, G_SSD, N_SSD)
    a_cs = jnp.cumsum(a, axis=-1)
    cb = jnp.einsum("bclgn,bcsgn->bgcls", cc, bc)
    mix = cb[:, :, None] * jnp.exp(_segsum(a))
    y_diag = jnp.einsum("bgecls,bcsgep->bclgep", mix, xf)
    decay_to_end = jnp.exp(a_cs[..., -1:] - a_cs)
    chunk_states = jnp.einsum("bclgn,bgecl,bclgep->bcgepn", bc, decay_to_end, xf)
    states = jnp.concatenate(
        [h0.astype(jnp.float32).reshape(b, G_SSD, E_SSD, P_SSD, N_SSD)[:, None], chunk_states], axis=1)
    chunk_tot = jnp.pad(a_cs[..., -1], ((0, 0), (0, 0), (0, 0), (1, 0)))
    states = jnp.einsum("bgezc,bcgepn->bzgepn", jnp.exp(_segsum(chunk_tot)), states)
    h_prev, h_last = states[:, :-1], states[:, -1]
    y_off = jnp.einsum("bclgn,bcgepn,bgecl->bclgep", cc, h_prev, jnp.exp(a_cs))
    y = (y_diag + y_off).reshape(b, t, H_SSD, P_SSD)
    return y.astype(x.dtype), h_last.reshape(b, H_SSD, P_SSD, N_SSD)


def _layer(x, past_k, past_v, ssm0, ssd_conv0, conf_conv0, ffn_conv0, p):
    b, t, _ = x.shape
    pos0 = past_k.shape[1]
    h = _rmsnorm(x, p["norm_mix_g"])
    proj = h @ p["w_in"]
    cuts = np.cumsum([D_ATT, D_ATT, D_ATT, D_SSD, SSD_CONV_DIM, H_SSD]).tolist()
    q, k, v, z, xbc, dt_raw, glu = jnp.split(proj, cuts, axis=-1)

    q = q.reshape(b, t, H_ATT, D_HEAD_ATT)
    k = k.reshape(b, t, H_ATT, D_HEAD_ATT)
    v = v.reshape(b, t, H_ATT, D_HEAD_ATT)
    k_all = jnp.concatenate([past_k.astype(k.dtype), k], axis=1)
    v_all = jnp.concatenate([past_v.astype(v.dtype), v], axis=1)
    att = _sb_attention(q, k_all, v_all, p["att_logit_bias"], pos0 + jnp.arange(t), jnp.arange(pos0 + t))
    att = _rmsnorm(att, p["att_norm_g"].reshape(H_ATT, D_HEAD_ATT)).reshape(b, t, D_ATT)

    xbc, ssd_conv_new = _causal_dwconv(xbc, ssd_conv0, p["ssd_conv_w"], p["ssd_conv_b"])
    xbc = jax.nn.silu(xbc)
    xs, bm, cm = jnp.split(xbc, [D_SSD, D_SSD + G_SSD * N_SSD], axis=-1)
    xs = xs.reshape(b, t, H_SSD, P_SSD)
    dt = jax.nn.softplus(dt_raw.astype(jnp.float32) + p["ssd_dt_bias"])
    a_head = -jnp.exp(p["ssd_a_log"].astype(jnp.float32))
    y, ssm_new = _ssd(xs, dt, a_head, bm.reshape(b, t, G_SSD, N_SSD), cm.reshape(b, t, G_SSD, N_SSD), ssm0)
    y = (y + p["ssd_d"][:, None] * xs).reshape(b, t, D_SSD) * jax.nn.silu(z)
    y = _rmsnorm(y.reshape(b, t, G_SSD, D_SSD // G_SSD),
                 p["ssd_norm_g"].reshape(G_SSD, D_SSD // G_SSD)).reshape(b, t, D_SSD)

    u, u_gate = jnp.split(glu, 2, axis=-1)
    u = u * jax.nn.sigmoid(u_gate)
    u, conf_conv_new = _causal_dwconv(u, conf_conv0, p["conf_conv_w"], p["conf_conv_b"])
    u = jax.nn.silu(_layernorm(u, p["conf_ln_g"], p["conf_ln_b"]))

    x = x + jnp.concatenate([att, y, u], axis=-1) @ p["w_out"]

    h = _rmsnorm(x, p["norm_ffn_g"])
    gate, val = jnp.split(h @ p["w_up"], 2, axis=-1)
    gate, ffn_conv_new = _causal_dwconv(gate, ffn_conv0, p["ffn_conv_w"], p["ffn_conv_b"])
    x = x + (jax.nn.silu(gate) * val) @ p["w_down"]
    return x, (k, v, ssm_new, ssd_conv_new, conf_conv_new, ffn_conv_new)


def setup_inputs(seed: int = 0) -> dict:
    key = jax.random.key(seed)
    ks = jax.random.split(key, 32)
    f32 = jnp.float32
    n_pages = PAST_LEN // PAGE_SIZE
    n_pool = (DEC_BATCH * n_pages * 5) // 4

    def nrm(i, shape, scale):
        return jax.random.normal(ks[i], shape, f32) * scale

    dt0 = jnp.exp(jax.random.uniform(ks[13], (DEPTH, H_SSD), f32, math.log(1e-3), math.log(1e-1)))
    page_table = jax.random.permutation(ks[8], n_pool)[: DEC_BATCH * n_pages]
    att_bias = jnp.linspace(-4.0, -8.0, H_ATT, dtype=f32)[None, :] + nrm(29, (DEPTH, H_ATT), 0.1)
    return {
        "x_prompt": nrm(0, (BATCH, SEQ, D_MODEL), 1.0),
        "x_sample": nrm(1, (DEC_BATCH, DEC_SEQ, D_MODEL), 1.0),
        "cache_k": nrm(2, (DEPTH, n_pool, PAGE_SIZE, H_ATT, D_HEAD_ATT), 1.0),
        "cache_v": nrm(3, (DEPTH, n_pool, PAGE_SIZE, H_ATT, D_HEAD_ATT), 1.0),
        "state_ssm": nrm(4, (DEPTH, DEC_BATCH, H_SSD, P_SSD, N_SSD), 0.1),
        "state_ssd_conv": nrm(5, (DEPTH, DEC_BATCH, SSD_CONV_W - 1, SSD_CONV_DIM), 1.0),
        "state_conf_conv": nrm(6, (DEPTH, DEC_BATCH, CONF_W - 1, C_CONF), 0.5),
        "state_ffn_conv": nrm(7, (DEPTH, DEC_BATCH, FFN_CONV_W - 1, D_FF), 1.0),
        "page_table": page_table.reshape(DEC_BATCH, n_pages).astype(jnp.int32),
        "norm_mix_g": 1.0 + nrm(9, (DEPTH, D_MODEL), 0.02),
        "w_in": nrm(10, (DEPTH, D_MODEL, D_IN), D_MODEL ** -0.5),
        "att_logit_bias": att_bias,
        "att_norm_g": 1.0 + nrm(11, (DEPTH, D_ATT), 0.02),
        "ssd_conv_w": nrm(12, (DEPTH, SSD_CONV_W, SSD_CONV_DIM), SSD_CONV_W ** -0.5),
        "ssd_conv_b": nrm(14, (DEPTH, SSD_CONV_DIM), 0.02),
        "ssd_dt_bias": dt0 + jnp.log(-jnp.expm1(-dt0)),
        "ssd_a_log": jnp.log(jax.random.uniform(ks[15], (DEPTH, H_SSD), f32, 1.0, 16.0)),
        "ssd_d": 1.0 + nrm(16, (DEPTH, H_SSD), 0.02),
        "ssd_norm_g": 1.0 + nrm(17, (DEPTH, D_SSD), 0.02),
        "conf_conv_w": nrm(18, (DEPTH, CONF_W, C_CONF), CONF_W ** -0.5),
        "conf_conv_b": nrm(19, (DEPTH, C_CONF), 0.02),
        "conf_ln_g": 1.0 + nrm(20, (DEPTH, C_CONF), 0.02),
        "conf_ln_b": nrm(21, (DEPTH, C_CONF), 0.02),
        "w_out": nrm(22, (DEPTH, D_MIX, D_MODEL), D_MIX ** -0.5),
        "norm_ffn_g": 1.0 + nrm(23, (DEPTH, D_MODEL), 0.02),
        "w_up": nrm(24, (DEPTH, D_MODEL, 2 * D_FF), D_MODEL ** -0.5),
        "ffn_conv_w": nrm(25, (DEPTH, FFN_CONV_W, D_FF), FFN_CONV_W ** -0.5),
        "ffn_conv_b": nrm(26, (DEPTH, D_FF), 0.02),
        "w_down": nrm(27, (DEPTH, D_FF, D_MODEL), D_FF ** -0.5),
        "norm_final_g": 1.0 + nrm(28, (D_MODEL,), 0.02),
    }


def reference(x_prompt, x_sample, cache_k, cache_v, state_ssm, state_ssd_conv, state_conf_conv,
              state_ffn_conv, page_table, norm_mix_g, w_in, att_logit_bias, att_norm_g, ssd_conv_w,
              ssd_conv_b, ssd_dt_bias, ssd_a_log, ssd_d, ssd_norm_g, conf_conv_w, conf_conv_b, conf_ln_g,
              conf_ln_b, w_out, norm_ffn_g, w_up, ffn_conv_w, ffn_conv_b, w_down, norm_final_g):
    n_pages = PAST_LEN // PAGE_SIZE
    n_seq = page_table.shape[0]
    xp, xs = x_prompt, x_sample
    bp = xp.shape[0]
    st_p, st_s = [], []
    for l in range(DEPTH):
        p = {
            "norm_mix_g": norm_mix_g[l], "w_in": w_in[l], "att_logit_bias": att_logit_bias[l],
            "att_norm_g": att_norm_g[l],
            "ssd_conv_w": ssd_conv_w[l], "ssd_conv_b": ssd_conv_b[l], "ssd_dt_bias": ssd_dt_bias[l],
            "ssd_a_log": ssd_a_log[l], "ssd_d": ssd_d[l], "ssd_norm_g": ssd_norm_g[l],
            "conf_conv_w": conf_conv_w[l], "conf_conv_b": conf_conv_b[l], "conf_ln_g": conf_ln_g[l],
            "conf_ln_b": conf_ln_b[l], "w_out": w_out[l], "norm_ffn_g": norm_ffn_g[l],
            "w_up": w_up[l], "ffn_conv_w": ffn_conv_w[l], "ffn_conv_b": ffn_conv_b[l], "w_down": w_down[l],
        }
        empty_kv = jnp.zeros((bp, 0, H_ATT, D_HEAD_ATT), xp.dtype)
        xp, sp = _layer(
            xp, empty_kv, empty_kv,
            jnp.zeros((bp, H_SSD, P_SSD, N_SSD), jnp.float32),
            jnp.zeros((bp, SSD_CONV_W - 1, SSD_CONV_DIM), xp.dtype),
            jnp.zeros((bp, CONF_W - 1, C_CONF), xp.dtype),
            jnp.zeros((bp, FFN_CONV_W - 1, D_FF), xp.dtype), p)
        st_p.append(sp)
        past_k = cache_k[l][page_table].reshape(n_seq, n_pages * PAGE_SIZE, H_ATT, D_HEAD_ATT)
        past_v = cache_v[l][page_table].reshape(n_seq, n_pages * PAGE_SIZE, H_ATT, D_HEAD_ATT)
        xs, ss = _layer(xs, past_k, past_v, state_ssm[l], state_ssd_conv[l], state_conf_conv[l],
                        state_ffn_conv[l], p)
        st_s.append(ss)
    y_prompt = _rmsnorm(xp, norm_final_g)
    y_sample = _rmsnorm(xs, norm_final_g)
    k_prompt = jnp.stack([s[0] for s in st_p])
    v_prompt = jnp.stack([s[1] for s in st_p])
    k_sample = jnp.stack([s[0] for s in st_s])
    v_sample = jnp.stack([s[1] for s in st_s])
    ssm_prompt = jnp.stack([s[2] for s in st_p])
    ssm_sample = jnp.stack([s[2] for s in st_s])
    ssd_conv_prompt = jnp.stack([s[3] for s in st_p])
    ssd_conv_sample = jnp.stack([s[3] for s in st_s])
    conf_conv_prompt = jnp.stack([s[4] for s in st_p])
    conf_conv_sample = jnp.stack([s[4] for s in st_s])
    ffn_conv_prompt = jnp.stack([s[5] for s in st_p])
    ffn_conv_sample = jnp.stack([s[5] for s in st_s])
    return (y_prompt, y_sample, k_prompt, v_prompt, k_sample, v_sample, ssm_prompt, ssm_sample,
            ssd_conv_prompt, ssd_conv_sample, conf_conv_prompt, conf_conv_sample,
            ffn_conv_prompt, ffn_conv_sample)
```

```python
import contextlib
import numpy as np
import concourse.bass as bass
import concourse.mybir as mybir
from concourse.bass_utils import run_bass_kernel_spmd

F32 = mybir.dt.float32
BF16 = mybir.dt.bfloat16
I32 = mybir.dt.int32
AF = mybir.ActivationFunctionType
ALU = mybir.AluOpType
AX = mybir.AxisListType

NCORES = 8
D = 2048
P_TOK = 2048
S_SEQ = 4
S_LEN = 8
S_TOK = S_SEQ * S_LEN
T = P_TOK + S_TOK
DEPTH = 2
D_ATT = 512
H_ATT = 8
DH = 64
D_SSD = 1024
H_SSD = 16
P_SSD = 64
N_SSD = 128
CONVD = 1536
C_CONF = 512
CONF_W = 31
D_FF = 5504
KC_FF = 43
D_IN = 5136
EPS = 1e-6
NPAGE = 64
NPOOL = 2560
NEG = -30000.0

OQ, OK_, OV, OZ, OXBC, ODT, OGLU = 0, 512, 1024, 1536, 2560, 4096, 4112

TOK_TILES = [(i * 128, 128) for i in range(16)] + [(P_TOK, S_TOK)]
TOK_BLOCKS = [(i * 512, 512) for i in range(4)] + [(P_TOK, S_TOK)]


class Dep:
    __slots__ = ("name", "w", "r", "excl")

    def __init__(self, name, frontier):
        self.name = name
        self.w = None
        self.r = list(frontier)
        self.excl = False


class KB:
    NSP = 40

    def __init__(self, nc):
        self.nc = nc
        self.eng = {"pe": nc.tensor, "act": nc.scalar, "dve": nc.vector, "pool": nc.gpsimd, "sp": nc.sync}
        self.sems = {}
        self.cnt = {}
        for e in ("pe", "act", "dve", "pool"):
            self.sems[e] = nc.alloc_semaphore("sem_" + e)
            self.cnt[e] = 0
        self.seen = {e: {} for e in self.eng}
        self.dq = {}
        for q, n in (("sp", KB.NSP), ("pool", 32), ("act", 8)):
            keys = []
            for i in range(n):
                k = "dma_%s_%d" % (q, i)
                self.sems[k] = nc.alloc_semaphore(k)
                self.cnt[k] = 0
                keys.append(k)
            self.dq[q] = {"keys": keys, "i": 0}
        self.frontier = {}
        self.scope_deps = []
        self.stack = contextlib.ExitStack()
        self.uid = 0
        self.rr = 0
        self.cast_engines = ("act", "dve")
        self.g1_parts = ("fm", "tm")
        self.dbg_ng = 99
        self.skip_vbf = False
        self.dbg_nt = 99

    def dep(self, name="d"):
        d = Dep(name, list(self.frontier.items()))
        if self.scope_deps:
            self.scope_deps[-1].append(d)
        return d

    @contextlib.contextmanager
    def scope(self):
        self.scope_deps.append([])
        with contextlib.ExitStack() as st:
            old = self.stack
            self.stack = st
            try:
                yield
            finally:
                self.stack = old
        deps = self.scope_deps.pop()
        for d in deps:
            toks = list(d.r)
            if d.w is not None:
                toks.append(d.w)
            for k, v in toks:
                if self.frontier.get(k, 0) < v:
                    self.frontier[k] = v

    def sb(self, shape, dtype, name="t"):
        self.uid += 1
        t = self.stack.enter_context(self.nc.sbuf_tensor("%s_%d" % (name, self.uid), list(shape), dtype))
        return t, self.dep(name)

    def _wait(self, e, tok):
        if tok is None:
            return
        k, v = tok
        if e == "pe" and k == "pe":
            return
        if self.seen[e].get(k, 0) >= v:
            return
        self.eng[e].wait_ge(self.sems[k], v)
        self.seen[e][k] = v

    def _pre(self, e, reads, writes, same_war=False):
        for d in reads:
            self._wait(e, d.w)
            if d.excl:
                for tk in d.r:
                    if tk[0] != e:
                        self._wait(e, tk)
        for d in writes:
            self._wait(e, d.w)
            for tk in d.r:
                if tk[0] == e:
                    continue
                self._wait(e, tk)

    def _post(self, tok, reads, writes):
        for d in reads:
            if d not in writes:
                d.r.append(tok)
                if len(d.r) > 24:
                    m = {}
                    for k, v in d.r:
                        if m.get(k, 0) < v:
                            m[k] = v
                    d.r = list(m.items())
        for d in writes:
            d.w = tok
            d.r = []

    def op(self, e, reads, writes, fn):
        self._pre(e, reads, writes)
        ins = fn(self.eng[e])
        self.cnt[e] += 1
        ins.then_inc(self.sems[e], 1)
        tok = (e, self.cnt[e])
        self._post(tok, reads, writes)
        return tok

    def dma(self, q, out, in_, reads, writes, **kw):
        dq = self.dq[q]
        k = dq["keys"][dq["i"] % len(dq["keys"])]
        dq["i"] += 1
        if self.cnt[k] > 0:
            self._wait(q, (k, self.cnt[k]))
        self._pre(q, reads, writes)
        ins = self.eng[q].dma_start(out=out, in_=in_, **kw)
        self.cnt[k] += 16
        ins.then_inc(self.sems[k], 16)
        tok = (k, self.cnt[k])
        self._post(tok, reads, writes)
        return tok

    def store(self, out, in_, reads, writes, **kw):
        return self.dma("pool", out, in_, reads, writes, **kw)

    def idma(self, out, in_, idx_ap, reads, writes):
        q = "pool"
        dq = self.dq[q]
        k = dq["keys"][dq["i"] % len(dq["keys"])]
        dq["i"] += 1
        if self.cnt[k] > 0:
            self._wait(q, (k, self.cnt[k]))
        self._pre(q, reads, writes)
        ins = self.eng[q].indirect_dma_start(out=out, out_offset=None, in_=in_,
                                             in_offset=bass.IndirectOffsetOnAxis(ap=idx_ap, axis=0))
        self.cnt[k] += 16
        ins.then_inc(self.sems[k], 16)
        tok = (k, self.cnt[k])
        self._post(tok, reads, writes)
        return tok

    def wait_all(self, e, deps):
        for d in deps:
            self._wait(e, d.w)
            for tk in d.r:
                self._wait(e, tk)

    def settle(self, deps):
        for e in self.eng:
            for d in deps:
                self._wait(e, d.w)

    def cast_engine(self):
        self.rr += 1
        return self.cast_engines[self.rr % len(self.cast_engines)]


def _copy(kb, e, out, in_, reads, writes):
    if e == "act":
        return kb.op("act", reads, writes, lambda en: en.copy(out=out, in_=in_))
    return kb.op(e, reads, writes, lambda en: en.tensor_copy(out=out, in_=in_))


class PSUM:
    def __init__(self, kb):
        self.t = []
        self.d = []
        for i in range(8):
            t = kb.nc.alloc_psum_tensor("psb%d" % i, [128, 512], F32)
            self.t.append(t)
            self.d.append(kb.dep("ps%d" % i))
            self.d[-1].excl = True


def gemm(kb, ps, banks, aT, aT_dep, kc0, nkc, w_ap, col_groups, mode, tok_list, evac, wpool):
    nc = kb.nc
    bi = [0]
    for gi, (c0, wd) in enumerate(col_groups):
        wt, wt_dep = wpool.get(nkc, wd)
        per = max(1, min(nkc, wpool.stage_elems // wd))
        k = 0
        while k < nkc:
            n = min(per, nkc - k)
            st, st_dep = wpool.stage()
            src = w_ap[(kc0 + k) * 128:(kc0 + k + n) * 128, c0:c0 + wd].rearrange("(k p) n -> p k n", p=128)
            kb.dma("sp", st[:, 0:n * wd].rearrange("p (k n) -> p k n", n=wd), src, [], [st_dep])
            ce = kb.cast_engine()
            _copy(kb, ce, wt[:, k:k + n, 0:wd], st[:, 0:n * wd].rearrange("p (k n) -> p k n", n=wd), [st_dep], [wt_dep])
            k += n
        if mode == "FM":
            sw = min(128, wd)
            for sub in range(max(1, wd // 128)):
                for (t0, tn) in tok_list:
                    b = banks[bi[0] % len(banks)]
                    bi[0] += 1
                    for kc in range(nkc):
                        kb.op("pe", [wt_dep, aT_dep], [ps.d[b]],
                              lambda en, kc=kc, b=b: en.matmul(ps.t[b][0:sw, 0:tn], lhsT=wt[:, kc, sub * 128:sub * 128 + sw],
                                                                rhs=aT[:, kc0 + kc, t0:t0 + tn],
                                                                start=(kc == 0), stop=(kc == nkc - 1)))
                    evac(ps.t[b][0:sw, 0:tn], ps.d[b], gi, sub, t0, tn)
        else:
            for (t0, tn) in tok_list:
                b = banks[bi[0] % len(banks)]
                bi[0] += 1
                for kc in range(nkc):
                    kb.op("pe", [wt_dep, aT_dep], [ps.d[b]],
                          lambda en, kc=kc, b=b: en.matmul(ps.t[b][0:tn, 0:wd], lhsT=aT[:, kc0 + kc, t0:t0 + tn],
                                                            rhs=wt[:, kc, 0:wd],
                                                            start=(kc == 0), stop=(kc == nkc - 1)))
                evac(ps.t[b][0:tn, 0:wd], ps.d[b], gi, 0, t0, tn)


class WPool:
    def __init__(self, kb, max_elems=8192, stage_elems=2048, nstage=3, nw=2):
        self.kb = kb
        self.stage_elems = stage_elems
        self.max_elems = max_elems
        self.st = [kb.sb([128, stage_elems], F32, "wstage") for _ in range(nstage)]
        self.w = [kb.sb([128, max_elems], BF16, "wbf") for _ in range(nw)]
        self.si = 0
        self.wi = 0

    def stage(self):
        t, d = self.st[self.si % len(self.st)]
        self.si += 1
        return t, d

    def get(self, nkc, wd):
        assert nkc * wd <= self.max_elems
        t, d = self.w[self.wi % len(self.w)]
        self.wi += 1
        return t[:, 0:nkc * wd].rearrange("p (k n) -> p k n", n=wd), d


def phase_norm(kb, ps, C, x_ap, x_dep, g_bc, g_dep, hT, hT_dep, banks, y_out=None, y_dep=None):
    with kb.scope():
        xts = [kb.sb([128, D], F32, "nx") for _ in range(2)]
        junk, junk_d = kb.sb([128, D], BF16, "njunk")
        hbs = [kb.sb([128, D], BF16 if y_out is None else F32, "nhb") for _ in range(2)]
        st = [kb.sb([128, 4], F32, "nst") for _ in range(2)]
        for i, (t0, tn) in enumerate(TOK_TILES):
            xt, xd = xts[i % 2]
            hb, hd = hbs[i % 2]
            s, sd = st[i % 2]
            kb.dma("sp", xt[0:tn, :], x_ap[t0:t0 + tn, :], [x_dep], [xd])
            kb.op("act", [xd], [junk_d, sd], lambda en: en.activation(out=junk[0:tn, :], in_=xt[0:tn, :], func=AF.Square,
                                                                       accum_out=s[0:tn, 0:1]))
            kb.op("act", [sd], [sd], lambda en: en.activation(out=s[0:tn, 1:2], in_=s[0:tn, 0:1], func=AF.Sqrt,
                                                               scale=1.0 / D, bias=C["eps"][0:tn, 0:1]))
            kb.op("dve", [sd], [sd], lambda en: en.reciprocal(out=s[0:tn, 2:3], in_=s[0:tn, 1:2]))
            kb.op("dve", [xd, sd, g_dep], [hd],
                  lambda en: en.scalar_tensor_tensor(out=hb[0:tn, :], in0=xt[0:tn, :], scalar=s[0:tn, 2:3], in1=g_bc[0:tn, :],
                                                     op0=ALU.mult, op1=ALU.mult))
            if y_out is not None:
                kb.store(y_out[t0:t0 + tn, :], hb[0:tn, :], [hd], [y_dep])
                continue
            for half in range(2):
                b = banks[(2 * i + half) % len(banks)]
                pb = ps.t[b][:].bitcast(BF16)
                for c in range(8):
                    cc = half * 8 + c
                    kb.op("pe", [hd, C["ident_bf_d"]], [ps.d[b]],
                          lambda en, c=c, cc=cc, pb=pb: en.transpose(out=pb[:, c * 128:c * 128 + tn], in_=hb[0:tn, cc * 128:(cc + 1) * 128],
                                                                   identity=C["ident_bf"][0:tn, 0:tn]))
                e = "act" if (i + half) % 2 == 0 else "dve"
                _copy(kb, e, hT[:, half * 8:(half + 1) * 8, t0:t0 + tn],
                      pb.rearrange("p (c t) -> p c t", t=128)[:, :, 0:tn], [ps.d[b]], [hT_dep])


def load_consts(kb, consts_ap, CW):
    C = {}
    cf, cfd = kb.sb([128, CW], F32, "constf")
    cb, cbd = kb.sb([128, CW], BF16, "constb")
    ind = kb.dep("consts_in")
    kb.dma("sp", cf[:], consts_ap[:, :], [ind], [cfd])
    kb.op("dve", [cfd], [cbd], lambda en: en.tensor_copy(out=cb[:], in_=cf[:]))
    kb.settle([cfd, cbd])
    C["f"] = cf
    C["f_d"] = cfd
    C["b"] = cb
    C["b_d"] = cbd
    C["ident_f"] = cf[:, 0:128]
    C["ident_bf"] = cb[:, 0:128]
    C["ident_bf_d"] = cbd
    C["tri_bf"] = cb[:, 128:256]
    C["mlt_bf"] = cb[:, 256:384]
    C["negm_f"] = cf[:, 384:512]
    C["ones_bf"] = cb[:, 512:640]
    C["ones_f"] = cf[:, 512:640]
    C["mle_f"] = cf[:, 640:768]
    C["eps"] = cf[:, 768:769]
    C["one"] = cf[:, 769:770]
    C["iota"] = cf[:, 770:771]
    C["mlt8_bf"] = cb[0:8, 772:836]
    C["negm8_f"] = cf[0:8, 836:900]
    C["bmask_f"] = cf[0:64, 1024:1536]
    return C


def make_consts():
    CW = 1536
    c = np.zeros((128, CW), np.float32)
    i = np.arange(128)
    c[:, 0:128] = np.eye(128)
    c[:, 128:256] = (i[:, None] > i[None, :])
    c[:, 256:384] = (i[:, None] < i[None, :])
    c[:, 384:512] = np.where(i[:, None] < i[None, :], 0.0, NEG)
    c[:, 512:640] = 1.0
    c[:, 640:768] = (i[:, None] <= i[None, :])
    c[:, 768] = EPS
    c[:, 769] = 1.0
    c[:, 770] = i
    qq = np.tile(np.arange(8), 8)[None, :]
    ii = np.arange(8)[:, None]
    c[0:8, 772:836] = (ii < qq)
    c[0:8, 836:900] = np.where(ii < qq, 0.0, NEG)
    hp = (np.arange(512) // 64)[None, :]
    c[0:64, 1024:1536] = (hp == (np.arange(64) // 8)[:, None])
    return c


def phase_gemm1(kb, ps, wpool, hT, hT_dep, w_in_l, L):
    with kb.scope():
        evs = [kb.sb([128, 512], F32, "g1ev") for _ in range(4)]
        evb = [kb.sb([128, 512], BF16, "g1evb") for _ in range(3)]
        ei = [0]
        eb = [0]

        def stage_out(psap, psd, rows, tn, dst_ap, dst_dep, extra=None):
            t, d = evs[ei[0] % 4]
            e = "act" if ei[0] % 2 == 0 else "dve"
            ei[0] += 1
            _copy(kb, e, t[0:rows, 0:tn], psap, [psd], [d])
            if extra is not None:
                extra(t[0:rows, 0:tn], d)
            kb.store(dst_ap, t[0:rows, 0:tn], [d], [dst_dep])

        def stage_out_bf(psap, psd, rows, tn, dst_ap, dst_dep):
            t, d = evb[eb[0] % 3]
            e = "act" if eb[0] % 2 == 0 else "dve"
            eb[0] += 1
            _copy(kb, e, t[0:rows, 0:tn], psap, [psd], [d])
            kb.store(dst_ap, t[0:rows, 0:tn], [d], [dst_dep])

        fm_groups = [(OQ, 256), (OQ + 256, 256), (OK_, 256), (OK_ + 256, 256)]
        fm_groups += [(OXBC + i * 256, 256) for i in range(6)]
        fm_groups += [(ODT, 16)]
        fm_groups += [(OGLU + i * 256, 256) for i in range(4)]

        def evac_fm(psap, psd, gi, sub, t0, tn):
            c0, wd = fm_groups[gi]
            col = c0 + sub * 128
            if col < OV:
                stage_out_bf(psap, psd, 128, tn, L["qkT"][col:col + 128, t0:t0 + tn], L["qkT_d"])
            elif col < ODT:
                r0 = col - OXBC
                stage_out(psap, psd, 128, tn, L["xbcT"][r0:r0 + 128, t0:t0 + tn], L["xbcT_d"])
            elif col < OGLU:
                stage_out(psap, psd, 16, tn, L["dtT"][0:16, t0:t0 + tn], L["dtT_d"])
            else:
                r0 = col - OGLU
                stage_out(psap, psd, 128, tn, L["gluT"][r0:r0 + 128, t0:t0 + tn], L["gluT_d"])

        gemm(kb, ps, [0, 1, 2, 3], hT, hT_dep, 0, 16, w_in_l, fm_groups, "FM", TOK_BLOCKS, evac_fm, wpool)

        tm_groups = [(OK_, 512), (OV, 512), (OZ, 512), (OZ + 512, 512)]

        def evac_tm(psap, psd, gi, sub, t0, tn):
            if gi == 0:
                stage_out(psap, psd, tn, 512, L["k_new"][t0:t0 + tn, :], L["k_new_d"])
            elif gi == 1:
                def ex(src, srcd):
                    t, d = evb[eb[0] % 3]
                    eb[0] += 1
                    kb.op("pool", [srcd], [d], lambda en: en.tensor_copy(out=t[0:tn, :], in_=src))
                    kb.store(L["vbf"][t0:t0 + tn, :], t[0:tn, :], [d], [L["vbf_d"]])
                stage_out(psap, psd, tn, 512, L["v_new"][t0:t0 + tn, :], L["v_new_d"], extra=ex)
            else:
                z0 = (gi - 2) * 512
                stage_out(psap, psd, tn, 512, L["z"][t0:t0 + tn, z0:z0 + 512], L["z_d"])

        gemm(kb, ps, [4, 5, 6, 7], hT, hT_dep, 0, 16, w_in_l, tm_groups, "TM", TOK_TILES, evac_tm, wpool)


def build(cfg=None):
    cfg = cfg or {}
    npool = cfg.get("npool", NPOOL)
    nc = bass.Bass("TRN2", target_bir_lowering=False)
    kb = KB(nc)
    ps = PSUM(kb)
    A = {}
    Dd = {}

    def dram(name, shape, dtype=F32, kind="Internal"):
        A[name] = nc.dram_tensor(name, list(shape), dtype, kind=kind).ap()
        Dd[name] = kb.dep(name)

    def dram_in(name, shape, dtype=F32):
        dram(name, shape, dtype, "ExternalInput")

    def dram_out(name, shape, dtype=F32):
        dram(name, shape, dtype, "ExternalOutput")

    CW = 1536
    dram_in("consts", [128, CW])
    dram_in("x_in", [T, D])
    dram_in("w_in", [DEPTH, D, D_IN])
    dram_in("w_out", [DEPTH, D, D])
    dram_in("w_up", [DEPTH, D, 2 * D_FF])
    dram_in("w_down", [DEPTH, D_FF, D])
    dram_in("norm_g", [DEPTH * 2 + 1, D])
    dram_in("att_logit_bias", [DEPTH, 8])
    dram_in("att_norm_g", [DEPTH, 512])
    dram_in("cache_k", [DEPTH, npool * 128, 512])
    dram_in("cache_v", [DEPTH, npool * 128, 512])
    dram_in("page_table", [S_SEQ, NPAGE], I32)
    dram_in("ssd_fm", [DEPTH, 128, 12 * 5])
    dram_in("ssd_h16", [DEPTH, 16, 2])
    dram_in("ssd_d", [DEPTH, 16])
    dram_in("ssd_norm_g", [DEPTH, 1024])
    dram_in("ssd_prev", [DEPTH, 128, 12, S_SEQ, 3])
    dram_in("ssm_h0T", [DEPTH, S_SEQ, 128, 1024])
    dram_in("conf_fm", [DEPTH, 128, 4 * 34])
    dram_in("conf_prev", [DEPTH, 128, 4, S_SEQ, 30])
    dram_in("ffn_fm", [DEPTH, 128, KC_FF * 4])
    dram_in("ffn_prev", [DEPTH, 128, KC_FF, S_SEQ, 2])
    dram_out("y", [T, D])
    dram_out("k_new", [DEPTH, T, 512])
    dram_out("v_new", [DEPTH, T, 512])
    dram_out("ssm", [DEPTH, 5, 16, 64, 128])
    dram_out("ssd_conv", [DEPTH, 5, 3, CONVD])
    dram_out("conf_conv", [DEPTH, 5, 30, 512])
    dram_out("ffn_conv", [DEPTH, 5, 2, D_FF])
    for l in range(DEPTH):
        dram("xbcT%d" % l, [CONVD, T])
        dram("xcs%d" % l, [CONVD, T])
        dram("dtT%d" % l, [16, T])
        dram("gluT%d" % l, [1024, T])
        dram("z%d" % l, [T, 1024])
        dram("qkT%d" % l, [1024, T], BF16)
        dram("vbf%d" % l, [T, 512], BF16)
        dram("x1_%d" % l, [T, D])
        dram("x2_%d" % l, [T, D])

    C = load_consts(kb, A["consts"], CW)
    pidx, pidx_d = build_page_idx(kb, C, A["page_table"], Dd["page_table"], S_SEQ, NPAGE, npool)
    ck_flat = A["cache_k"].rearrange("l r c -> (l r) c")
    cv_flat = A["cache_v"].rearrange("l r c -> (l r) c")
    x_cur, x_cur_d = A["x_in"], Dd["x_in"]

    def bcast_load(dst, dst_d, src_ap, src_d, n=128):
        kb.dma("sp", dst, src_ap.partition_broadcast(n), [src_d], [dst_d])

    for l in range(DEPTH):
        L = {}
        for nm in ("xbcT", "xcs", "dtT", "gluT", "z", "qkT", "vbf"):
            L[nm] = A["%s%d" % (nm, l)]
            L[nm + "_d"] = Dd["%s%d" % (nm, l)]
        L["k_new"], L["k_new_d"] = A["k_new"][l], Dd["k_new"]
        L["v_new"], L["v_new_d"] = A["v_new"][l], Dd["v_new"]
        x1, x1_d = A["x1_%d" % l], Dd["x1_%d" % l]
        x2, x2_d = A["x2_%d" % l], Dd["x2_%d" % l]
        with kb.scope():
            hT, hT_d = kb.sb([128, 16, T], BF16, "hT")
            g_bc, g_d = kb.sb([128, D], F32, "gbc")
            bcast_load(g_bc[:], g_d, A["norm_g"][2 * l], Dd["norm_g"])
            phase_norm(kb, ps, C, x_cur, x_cur_d, g_bc, g_d, hT, hT_d, [0, 1, 2, 3])
            with kb.scope():
                wpool = WPool(kb)
                phase_gemm1(kb, ps, wpool, hT, hT_d, A["w_in"][l], L)
        with kb.scope():
            mixT, mixT_d = kb.sb([128, 16, T], BF16, "mixT")
            with kb.scope():
                qT, qT_d = kb.sb([128, 4, T], BF16, "qT")
                kT, kT_d = kb.sb([128, 4, T], BF16, "kT")
                v_bf, v_bf_d = kb.sb([128, 16, 512], BF16, "vbf")
                kb.dma("sp", qT[:], L["qkT"][0:512, :].rearrange("(c p) t -> p c t", p=128), [L["qkT_d"]], [qT_d])
                kb.dma("sp", kT[:], L["qkT"][512:1024, :].rearrange("(c p) t -> p c t", p=128), [L["qkT_d"]], [kT_d])
                kb.dma("sp", v_bf[:], L["vbf"][0:P_TOK, :].rearrange("(c p) n -> p c n", p=128), [L["vbf_d"]], [v_bf_d])
                sm, sm_d = kb.sb([128, 8 + 512], F32, "attsm")
                bcast_load(sm[:, 0:8], sm_d, A["att_logit_bias"][l], Dd["att_logit_bias"])
                bcast_load(sm[:, 8:520], sm_d, A["att_norm_g"][l], Dd["att_norm_g"])
                g_hq, g_hq_d = kb.sb([64, 64], F32, "ghq")
                for h in range(8):
                    kb.dma("sp", g_hq[h * 8:(h + 1) * 8, :], A["att_norm_g"][l, h * 64:(h + 1) * 64].partition_broadcast(8),
                           [Dd["att_norm_g"]], [g_hq_d])
                if not cfg.get("skip_attn_p"):
                    phase_attn_prompt(kb, ps, C, qT, qT_d, kT, kT_d, v_bf, v_bf_d, sm[:, 0:8], sm_d, sm[:, 8:520], sm_d, mixT, mixT_d)
                if not cfg.get("skip_attn_s"):
                    phase_attn_sample(kb, ps, C, qT, qT_d, kT, kT_d, L["v_new"], L["v_new_d"], ck_flat, cv_flat,
                                      Dd["cache_k"], pidx[:, l, :], pidx_d, sm[:, 0:8], sm_d, g_hq, g_hq_d, mixT, mixT_d)
            with kb.scope():
                cf, cf_d = kb.sb([128, 4 * 34], F32, "conffm")
                kb.dma("sp", cf[:], A["conf_fm"][l], [Dd["conf_fm"]], [cf_d])
                phase_conf(kb, ps, C, L["gluT"], L["gluT_d"], cf[:, 0:124].rearrange("p (c k) -> p c k", k=31), cf[:, 124:128],
                           cf[:, 128:132], cf[:, 132:136], cf_d, A["conf_prev"][l], Dd["conf_prev"], mixT, mixT_d,
                           A["conf_conv"][l], Dd["conf_conv"])
            with kb.scope():
                sf, sf_d = kb.sb([128, 60], F32, "ssdfm")
                kb.dma("sp", sf[:], A["ssd_fm"][l], [Dd["ssd_fm"]], [sf_d])
                s16, s16_d = kb.sb([16, 2], F32, "ssd16")
                kb.dma("sp", s16[:], A["ssd_h16"][l], [Dd["ssd_h16"]], [s16_d])
                kb.settle([sf_d, s16_d])
                sbc, sbc_d = kb.sb([128, 16 + 1024], F32, "ssdbc")
                bcast_load(sbc[:, 0:16], sbc_d, A["ssd_d"][l], Dd["ssd_d"])
                bcast_load(sbc[:, 16:1040], sbc_d, A["ssd_norm_g"][l], Dd["ssd_norm_g"])
                phase_ssd(kb, ps, C, L, sf[:, 0:48].rearrange("p (c k) -> p c k", k=4), sf[:, 48:60], sf_d, s16[:, 0:1], s16[:, 1:2],
                          sbc[:, 0:16], sbc[:, 16:1040], sbc_d, A["ssd_prev"][l], Dd["ssd_prev"], A["ssm_h0T"][l], Dd["ssm_h0T"],
                          mixT, mixT_d, A["ssm"][l], Dd["ssm"], A["ssd_conv"][l], Dd["ssd_conv"])
            with kb.scope():
                wpool = WPool(kb)
                phase_gemm_resid(kb, ps, wpool, mixT, mixT_d, 0, 16, A["w_out"][l], x_cur, x_cur_d, x1, x1_d, [0, 1, 2, 3])
        with kb.scope():
            h2T, h2T_d = kb.sb([128, 16, T], BF16, "h2T")
            with kb.scope():
                g_bc, g_d = kb.sb([128, D], F32, "gbc2")
                bcast_load(g_bc[:], g_d, A["norm_g"][2 * l + 1], Dd["norm_g"])
                phase_norm(kb, ps, C, x1, x1_d, g_bc, g_d, h2T, h2T_d, [0, 1, 2, 3])
            ff, ff_d = kb.sb([128, KC_FF * 4], F32, "ffnfm")
            kb.dma("sp", ff[:], A["ffn_fm"][l], [Dd["ffn_fm"]], [ff_d])
            fp_, fp_d = kb.sb([128, KC_FF, S_SEQ, 2], F32, "ffnprev")
            kb.dma("sp", fp_[:], A["ffn_prev"][l], [Dd["ffn_prev"]], [fp_d])
            wpool = WPool(kb, max_elems=4096)
            phase_ffn(kb, ps, C, wpool, h2T, h2T_d, A["w_up"][l], A["w_down"][l], ff[:, 0:129].rearrange("p (c k) -> p c k", k=3),
                      ff[:, 129:172], ff_d, fp_, fp_d, x1, x1_d, x2, x2_d, A["ffn_conv"][l], Dd["ffn_conv"])
        x_cur, x_cur_d = x2, x2_d

    with kb.scope():
        g_bc, g_d = kb.sb([128, D], F32, "gbcf")
        bcast_load(g_bc[:], g_d, A["norm_g"][2 * DEPTH], Dd["norm_g"])
        phase_norm(kb, ps, C, x_cur, x_cur_d, g_bc, g_d, None, None, [0, 1], y_out=A["y"], y_dep=Dd["y"])

    kb.wait_all("sp", list(Dd.values()))
    return nc


def _fm(a, nch):
    a = np.asarray(a)
    lead = a.shape[:-1]
    b = a.reshape(lead + (nch, 128))
    b = np.moveaxis(b, -1, 0)
    b = np.moveaxis(b, -1, 1)
    return np.ascontiguousarray(b)


def make_in_maps(inp):
    f = lambda k: np.asarray(inp[k])
    consts = make_consts()
    norm_g = np.stack([f("norm_mix_g")[0], f("norm_ffn_g")[0], f("norm_mix_g")[1], f("norm_ffn_g")[1], f("norm_final_g")]).astype(np.float32)
    ssd_fm = np.zeros((DEPTH, 128, 60), np.float32)
    conf_fm = np.zeros((DEPTH, 128, 136), np.float32)
    ffn_fm = np.zeros((DEPTH, 128, KC_FF * 4), np.float32)
    ssd_h16 = np.zeros((DEPTH, 16, 2), np.float32)
    for l in range(DEPTH):
        ssd_fm[l, :, 0:48] = _fm(f("ssd_conv_w")[l], 12).reshape(128, 48)
        ssd_fm[l, :, 48:60] = _fm(f("ssd_conv_b")[l], 12)
        conf_fm[l, :, 0:124] = _fm(f("conf_conv_w")[l], 4).reshape(128, 124)
        conf_fm[l, :, 124:128] = _fm(f("conf_conv_b")[l], 4)
        conf_fm[l, :, 128:132] = _fm(f("conf_ln_g")[l], 4)
        conf_fm[l, :, 132:136] = _fm(f("conf_ln_b")[l], 4)
        ffn_fm[l, :, 0:129] = _fm(f("ffn_conv_w")[l], KC_FF).reshape(128, 129)
        ffn_fm[l, :, 129:172] = _fm(f("ffn_conv_b")[l], KC_FF)
        ssd_h16[l, :, 0] = f("ssd_dt_bias")[l]
        ssd_h16[l, :, 1] = f("ssd_a_log")[l]
    ck = f("cache_k").reshape(DEPTH, -1, 512)
    cv = f("cache_v").reshape(DEPTH, -1, 512)
    maps = []
    for c in range(NCORES):
        sl = slice(c * S_SEQ, (c + 1) * S_SEQ)
        x_in = np.concatenate([f("x_prompt")[c], f("x_sample")[sl].reshape(S_TOK, D)], 0)
        m = {
            "consts": consts, "x_in": np.ascontiguousarray(x_in),
            "w_in": f("w_in"), "w_out": f("w_out"), "w_up": f("w_up"), "w_down": f("w_down"),
            "norm_g": norm_g, "att_logit_bias": f("att_logit_bias"), "att_norm_g": f("att_norm_g"),
            "cache_k": ck, "cache_v": cv, "page_table": np.ascontiguousarray(f("page_table")[sl]).astype(np.int32),
            "ssd_fm": ssd_fm, "ssd_h16": ssd_h16, "ssd_d": f("ssd_d"), "ssd_norm_g": f("ssd_norm_g"),
            "ssd_prev": np.ascontiguousarray(np.stack([_fm(np.moveaxis(f("state_ssd_conv")[l, sl], -1, -1), 12) for l in range(DEPTH)])),
            "ssm_h0T": np.ascontiguousarray(f("state_ssm")[:, sl].reshape(DEPTH, S_SEQ, 1024, 128).transpose(0, 1, 3, 2)),
            "conf_fm": conf_fm,
            "conf_prev": np.ascontiguousarray(np.stack([_fm(f("state_conf_conv")[l, sl], 4) for l in range(DEPTH)])),
            "ffn_fm": ffn_fm,
            "ffn_prev": np.ascontiguousarray(np.stack([_fm(f("state_ffn_conv")[l, sl], KC_FF) for l in range(DEPTH)])),
        }
        maps.append(m)
    return maps


def assemble(results):
    cat = lambda k, sl_: np.stack([r[k][sl_] for r in results])
    y_prompt = np.stack([r["y"][0:P_TOK] for r in results])
    y_sample = np.concatenate([r["y"][P_TOK:T].reshape(S_SEQ, S_LEN, D) for r in results], 0)

    def kv(name):
        p = np.stack([r[name][:, 0:P_TOK].reshape(DEPTH, P_TOK, H_ATT, DH) for r in results], 1)
        s_ = np.concatenate([r[name][:, P_TOK:T].reshape(DEPTH, S_SEQ, S_LEN, H_ATT, DH) for r in results], 1)
        return p, s_

    k_p, k_s = kv("k_new")
    v_p, v_s = kv("v_new")

    def st(name):
        p = np.stack([r[name][:, 0] for r in results], 1)
        s_ = np.concatenate([r[name][:, 1:5] for r in results], 1)
        return p, s_

    ssm_p, ssm_s = st("ssm")
    sc_p, sc_s = st("ssd_conv")
    cc_p, cc_s = st("conf_conv")
    fc_p, fc_s = st("ffn_conv")
    outs = (y_prompt, y_sample, k_p, v_p, k_s, v_s, ssm_p, ssm_s, sc_p, sc_s, cc_p, cc_s, fc_p, fc_s)
    return tuple(np.ascontiguousarray(o, dtype=np.float32) for o in outs)


def kernel(**inputs):
    nc = build()
    in_maps = make_in_maps(inputs)
    res = run_bass_kernel_spmd(nc, in_maps, core_ids=list(range(NCORES)))
    return assemble(res.results)


def phase_attn_prompt(kb, ps, C, qT, qT_d, kT, kT_d, v_bf, v_bf_d, bias_bc, bias_d, attg_bc, attg_d, mixT, mixT_d,
                      nqb=16):
    with kb.scope():
        e_sb = [kb.sb([128, 512], F32, "ae") for _ in range(2)]
        sp_bf = [kb.sb([128, 512], BF16, "asp") for _ in range(2)]
        t1s = [kb.sb([128, 512], F32, "at1") for _ in range(2)]
        w_bf = [kb.sb([128, 512], BF16, "aw") for _ in range(2)]
        Rb = [kb.sb([128, 128], F32, "aR") for _ in range(2)]
        a_sb, a_d = kb.sb([128, 512], F32, "aa")
        sq, sq_d = kb.sb([128, 512], F32, "asq")
        ss, ss_d = kb.sb([128, 16], F32, "ass")
        an_bf, an_d = kb.sb([128, 512], BF16, "aan")
        bi = 0
        BA, BX = 6, 7
        for qb in range(nqb):
            q0 = qb * 128
            for h in range(8):
                c, pb = h // 2, (h % 2) * 64
                kts = list(range(qb, -1, -1))
                batches = [kts[i:i + 4] for i in range(0, len(kts), 4)]
                Rt, Rd = Rb[h % 2]
                have_R = False
                for bidx, batch in enumerate(batches):
                    nb = len(batch)
                    W = nb * 128
                    j = bi % 2
                    bi += 1
                    bS, bT, bC = j, 2 + j, 4 + j
                    et, ed = e_sb[j]
                    spt, spd = sp_bf[j]
                    t1, t1d = t1s[j]
                    wt, wd = w_bf[j]
                    for i, kt in enumerate(batch):
                        kb.op("pe", [kT_d, qT_d], [ps.d[bS]],
                              lambda en: en.matmul(ps.t[bS][:, i * 128:(i + 1) * 128], lhsT=kT[pb:pb + 64, c, kt * 128:(kt + 1) * 128],
                                                   rhs=qT[pb:pb + 64, c, q0:q0 + 128], start=True, stop=True))
                    kb.op("act", [ps.d[bS], bias_d], [ed],
                          lambda en: en.activation(out=et[:, 0:W], in_=ps.t[bS][:, 0:W], func=AF.Exp, scale=0.125,
                                                   bias=bias_bc[:, h:h + 1]))
                    kb.op("act", [ed], [spd],
                          lambda en: en.activation(out=spt[:, 0:W], in_=et[:, 0:W], func=AF.Ln, scale=1.0, bias=C["one"]))
                    diag = (bidx == 0)
                    if diag:
                        kb.op("dve", [spd], [spd],
                              lambda en: en.tensor_tensor(out=spt[:, 0:128], in0=spt[:, 0:128], in1=C["mlt_bf"], op=ALU.mult))
                    kb.op("pe", [spd], [ps.d[bT]],
                          lambda en: en.matmul(ps.t[bT][:, 0:W], lhsT=C["tri_bf"], rhs=spt[:, 0:W], start=True, stop=True))
                    kb.op("pe", [spd], [ps.d[bC]],
                          lambda en: en.matmul(ps.t[bC][:, 0:W], lhsT=C["ones_bf"], rhs=spt[:, 0:W], start=True, stop=True))
                    kb.op("dve", [ps.d[bS], spd], [t1d],
                          lambda en: en.scalar_tensor_tensor(out=t1[:, 0:W], in0=ps.t[bS][:, 0:W], scalar=0.125, in1=spt[:, 0:W],
                                                             op0=ALU.mult, op1=ALU.subtract))
                    kb.op("dve", [ps.d[bT], t1d], [t1d],
                          lambda en: en.tensor_tensor(out=t1[:, 0:W], in0=t1[:, 0:W], in1=ps.t[bT][:, 0:W], op=ALU.subtract))
                    for i in range(nb):
                        sl = slice(i * 128, (i + 1) * 128)
                        if have_R:
                            kb.op("dve", [t1d, Rd], [t1d],
                                  lambda en: en.tensor_tensor(out=t1[:, sl], in0=t1[:, sl], in1=Rt[:], op=ALU.subtract))
                        is_last = (bidx == len(batches) - 1 and i == nb - 1)
                        if not is_last:
                            if not have_R:
                                kb.op("dve", [ps.d[bC]], [Rd], lambda en: en.tensor_copy(out=Rt[:], in_=ps.t[bC][:, sl]))
                                have_R = True
                            else:
                                kb.op("dve", [ps.d[bC], Rd], [Rd],
                                      lambda en: en.tensor_tensor(out=Rt[:], in0=Rt[:], in1=ps.t[bC][:, sl], op=ALU.add))
                    if diag:
                        kb.op("dve", [t1d], [t1d],
                              lambda en: en.tensor_tensor(out=t1[:, 0:128], in0=t1[:, 0:128], in1=C["negm_f"], op=ALU.add))
                    kb.op("act", [t1d, bias_d], [wd],
                          lambda en: en.activation(out=wt[:, 0:W], in_=t1[:, 0:W], func=AF.Exp, scale=1.0, bias=bias_bc[:, h:h + 1]))
                    for i, kt in enumerate(batch):
                        first = (bidx == 0 and i == 0)
                        last = (bidx == len(batches) - 1 and i == nb - 1)
                        kb.op("pe", [wd, v_bf_d], [ps.d[BA]],
                              lambda en: en.matmul(ps.t[BA][:, h * 64:(h + 1) * 64], lhsT=wt[:, i * 128:(i + 1) * 128],
                                                   rhs=v_bf[:, kt, h * 64:(h + 1) * 64], start=first, stop=last))
            kb.op("act", [ps.d[BA]], [a_d], lambda en: en.copy(out=a_sb[:], in_=ps.t[BA][:]))
            kb.op("dve", [a_d], [sq_d], lambda en: en.tensor_tensor(out=sq[:], in0=a_sb[:], in1=a_sb[:], op=ALU.mult))
            kb.op("dve", [sq_d], [ss_d],
                  lambda en: en.tensor_reduce(out=ss[:, 0:8], in_=sq[:].rearrange("p (h d) -> p h d", d=64), axis=AX.X, op=ALU.add))
            kb.op("act", [ss_d], [ss_d],
                  lambda en: en.activation(out=ss[:, 8:16], in_=ss[:, 0:8], func=AF.Sqrt, scale=1.0 / 64, bias=C["eps"]))
            kb.op("dve", [ss_d], [ss_d], lambda en: en.reciprocal(out=ss[:, 0:8], in_=ss[:, 8:16]))
            for h in range(8):
                hs = slice(h * 64, (h + 1) * 64)
                kb.op("dve", [a_d, ss_d, attg_d], [an_d],
                      lambda en: en.scalar_tensor_tensor(out=an_bf[:, hs], in0=a_sb[:, hs], scalar=ss[:, h:h + 1], in1=attg_bc[:, hs],
                                                         op0=ALU.mult, op1=ALU.mult))
            pbx = ps.t[BX][:].bitcast(BF16)
            for c4 in range(4):
                kb.op("pe", [an_d], [ps.d[BX]],
                      lambda en: en.transpose(out=pbx[:, c4 * 128:(c4 + 1) * 128], in_=an_bf[:, c4 * 128:(c4 + 1) * 128],
                                              identity=C["ident_bf"]))
            kb.op("act", [ps.d[BX]], [mixT_d],
                  lambda en: en.copy(out=mixT[:, 0:4, q0:q0 + 128], in_=pbx[:, 0:512].rearrange("p (c t) -> p c t", t=128)))


def phase_gemm_resid(kb, ps, wpool, aT, aT_d, kc0, nkc, w_ap, xprev, xprev_d, xnext, xnext_d, banks):
    with kb.scope():
        xin = [kb.sb([128, 512], F32, "rxin") for _ in range(3)]
        xo = [kb.sb([128, 512], F32, "rxo") for _ in range(3)]
        ei = [0]
        groups = [(i * 512, 512) for i in range(4)]

        def evac(psap, psd, gi, sub, t0, tn):
            i = ei[0]
            ei[0] += 1
            xi, xid = xin[i % 3]
            xt, xtd = xo[i % 3]
            c0 = gi * 512
            kb.dma("sp", xi[0:tn, :], xprev[t0:t0 + tn, c0:c0 + 512], [xprev_d], [xid])
            kb.op("dve", [psd, xid], [xtd], lambda en: en.tensor_tensor(out=xt[0:tn, :], in0=psap, in1=xi[0:tn, :], op=ALU.add))
            kb.store(xnext[t0:t0 + tn, c0:c0 + 512], xt[0:tn, :], [xtd], [xnext_d])

        gemm(kb, ps, banks, aT, aT_d, kc0, nkc, w_ap, groups, "TM", TOK_TILES, evac, wpool)


FFN_PARTS = [(0, 8), (8, 8), (16, 8), (24, 8), (32, 8), (40, 3)]
GP_S0 = 2 + P_TOK
GP_W = GP_S0 + S_SEQ * (2 + S_LEN)


def phase_ffn(kb, ps, C, wpool, h2T, h2T_d, w_up_l, w_down_l, fcw, fcb, fprm_d, fprev, fprev_d, x1, x1_d, x2, x2_d,
              ffn_out, ffn_out_d):
    with kb.scope():
        tail, tail_d = kb.sb([128, KC_FF, 5, 2], F32, "ftail")
        gp = [kb.sb([128, GP_W], F32, "fgp") for _ in range(2)]
        vs = [kb.sb([128, T], F32, "fval") for _ in range(2)]
        acc, acc_d = kb.sb([128, T], F32, "facc")
        for g_, gd_ in gp:
            kb.op("pool", [], [gd_], lambda en: en.memset(g_[:, 0:2], 0.0))
        xsrc, xsrc_d = x1, x1_d
        jn = 0
        for (j0, nj) in FFN_PARTS:
            with kb.scope():
                aT, aT_d = kb.sb([128, nj, T], BF16, "faT")
                for jj in range(nj):
                    j = j0 + jj
                    gpt, gpd = gp[jn % 2]
                    vt, vd = vs[jn % 2]
                    jn += 1
                    kb.op("pool", [fprev_d], [gpd],
                          lambda en: en.tensor_copy(out=gpt[:, GP_S0:GP_W].rearrange("p (s w) -> p s w", w=10)[:, :, 0:2],
                                                    in_=fprev[:, j, :, :]))

                    def evac(psap, psd, gi, sub, t0, tn):
                        e = "act" if gi == 0 else "dve"
                        if gi == 0:
                            if t0 < P_TOK:
                                dst = gpt[:, 2 + t0:2 + t0 + tn]
                                src = psap
                            else:
                                dst = gpt[:, GP_S0:GP_W].rearrange("p (s w) -> p s w", w=10)[:, :, 2:10]
                                src = psap.rearrange("p (s w) -> p s w", w=S_LEN)
                            _copy(kb, e, dst, src, [psd], [gpd])
                        else:
                            _copy(kb, e, vt[:, t0:t0 + tn], psap, [psd], [vd])

                    gemm(kb, ps, [0, 1, 2, 3], h2T, h2T_d, 0, 16, w_up_l, [(j * 128, 128), (D_FF + j * 128, 128)], "FM",
                         TOK_BLOCKS, evac, wpool)
                    for (a0, n_, src_fn) in ((0, P_TOK, lambda k: gpt[:, k:k + P_TOK]),
                                             (P_TOK, S_TOK, lambda k: gpt[:, GP_S0:GP_W].rearrange("p (s w) -> p s w", w=10)[:, :, k:k + 8])):
                        dst = acc[:, a0:a0 + n_]
                        if a0 == P_TOK:
                            dst = dst.rearrange("p (s w) -> p s w", w=S_LEN)
                        kb.op("dve", [gpd, fprm_d], [acc_d],
                              lambda en: en.tensor_scalar(out=dst, in0=src_fn(0), scalar1=fcw[:, j, 0:1], scalar2=fcb[:, j:j + 1],
                                                          op0=ALU.mult, op1=ALU.add))
                        for k in (1, 2):
                            kb.op("dve", [gpd, fprm_d, acc_d], [acc_d],
                                  lambda en: en.scalar_tensor_tensor(out=dst, in0=src_fn(k), scalar=fcw[:, j, k:k + 1], in1=dst,
                                                                     op0=ALU.mult, op1=ALU.add))
                    kb.op("pool", [gpd], [tail_d], lambda en: en.tensor_copy(out=tail[:, j, 0, :], in_=gpt[:, P_TOK:P_TOK + 2]))
                    kb.op("pool", [gpd], [tail_d],
                          lambda en: en.tensor_copy(out=tail[:, j, 1:5, :],
                                                    in_=gpt[:, GP_S0:GP_W].rearrange("p (s w) -> p s w", w=10)[:, :, 8:10]))
                    kb.op("act", [acc_d], [acc_d], lambda en: en.activation(out=acc[:], in_=acc[:], func=AF.Silu))
                    kb.op("dve", [acc_d, vd], [aT_d], lambda en: en.tensor_tensor(out=aT[:, jj, :], in0=acc[:], in1=vt[:], op=ALU.mult))
                phase_gemm_resid(kb, ps, wpool, aT, aT_d, 0, nj, w_down_l[j0 * 128:(j0 + nj) * 128, :], xsrc, xsrc_d, x2, x2_d,
                                 [4, 5, 6, 7])
                xsrc, xsrc_d = x2, x2_d
        tt_, tt_d = kb.sb([16, D_FF], F32, "ftailtm")
        for j in range(KC_FF):
            b = j % 4
            kb.op("pe", [tail_d], [ps.d[b]],
                  lambda en: en.transpose(out=ps.t[b][0:10, 0:128], in_=tail[:, j, :, :].rearrange("p s w -> p (s w)"),
                                          identity=C["ident_f"]))
            _copy(kb, "act" if j % 2 == 0 else "dve", tt_[0:10, j * 128:(j + 1) * 128], ps.t[b][0:10, 0:128], [ps.d[b]], [tt_d])
        kb.store(ffn_out.rearrange("s w c -> (s w) c"), tt_[0:10, :], [tt_d], [ffn_out_d])


def phase_attn_sample(kb, ps, C, qT, qT_d, kT, kT_d, v_new_l, v_new_d, ck_l, cv_l, cache_d, idx, idx_d, bias_bc, bias_d,
                      g_hq, g_hq_d, mixT, mixT_d, npage=NPAGE, nseq=S_SEQ):
    with kb.scope():
        biasrow, br_d = kb.sb([128, 512], F32, "sbr")
        ones_t, on_d = kb.sb([128, 64], F32, "sones")
        kb.op("dve", [], [on_d], lambda en: en.memset(ones_t[:], 1.0))
        brv = biasrow[:].rearrange("p (s h q) -> p s h q", h=8, q=8)
        for h in range(8):
            kb.op("dve", [on_d, bias_d], [br_d],
                  lambda en: en.tensor_scalar(out=brv[:, :, h, :], in0=ones_t[:].rearrange("p (s q) -> p s q", q=8),
                                              scalar1=bias_bc[:, h:h + 1], scalar2=None, op0=ALU.mult))
        kpg = [kb.sb([128, 512], F32, "skp") for _ in range(3)]
        vpg = [kb.sb([128, 512], F32, "svp") for _ in range(3)]
        ktb = [kb.sb([128, 512], BF16, "skt") for _ in range(3)]
        vbf = [kb.sb([128, 8, 512], BF16, "svb") for _ in range(2)]
        z_sb = [kb.sb([128, 512], F32, "sz") for _ in range(2)]
        e_sb = [kb.sb([128, 512], F32, "se") for _ in range(2)]
        sp_bf = [kb.sb([128, 512], BF16, "ssp") for _ in range(2)]
        w_bf = [kb.sb([128, 512], BF16, "sw") for _ in range(2)]
        Rt, Rd = kb.sb([128, 64], F32, "sR")
        qblk, qb_d = kb.sb([128, 4, 16], BF16, "sqb")
        vn, vn_d = kb.sb([8, 512], F32, "svn")
        vnb, vnb_d = kb.sb([8, 512], BF16, "svnb")
        a_sb, a_d = kb.sb([64, 512], F32, "sa")
        a2, a2_d = kb.sb([64, 64], F32, "sa2")
        sq, sq_d = kb.sb([64, 64], F32, "ssq")
        st, st_d = kb.sb([64, 4], F32, "sst")
        an, an_d = kb.sb([64, 128], BF16, "san")
        BS, BT, BC, BA, BK, BX = (0, 1), (2,), (3,), 4, (5, 6), 7
        kb.op("pool", [], [qb_d], lambda en: en.memset(qblk[:], 0.0))
        bi = 0
        pgi = 0
        for s in range(nseq):
            c0 = P_TOK + s * S_LEN
            for c in range(4):
                kb.op("dve", [qT_d], [qb_d], lambda en: en.tensor_copy(out=qblk[0:64, c, 0:8], in_=qT[0:64, c, c0:c0 + 8]))
                kb.op("dve", [qT_d], [qb_d], lambda en: en.tensor_copy(out=qblk[64:128, c, 8:16], in_=qT[64:128, c, c0:c0 + 8]))
            kb.dma("sp", vn[:], v_new_l[c0:c0 + 8, :], [v_new_d], [vn_d])
            kb.op("dve", [vn_d], [vnb_d], lambda en: en.tensor_copy(out=vnb[:], in_=vn[:]))
            j = bi % 2
            bi += 1
            bS = BS[j]
            zt, zd = z_sb[j]
            et, ed = e_sb[j]
            spt, spd = sp_bf[j]
            wt, wd = w_bf[j]
            for c in range(4):
                kb.op("pe", [kT_d, qb_d], [ps.d[bS]],
                      lambda en: en.matmul(ps.t[bS][0:8, c * 16:(c + 1) * 16], lhsT=kT[:, c, c0:c0 + 8], rhs=qblk[:, c, :],
                                           start=True, stop=True))
            kb.op("dve", [ps.d[bS], br_d], [zd],
                  lambda en: en.scalar_tensor_tensor(out=zt[0:8, 0:64], in0=ps.t[bS][0:8, 0:64], scalar=0.125, in1=biasrow[0:8, 0:64],
                                                     op0=ALU.mult, op1=ALU.add))
            kb.op("act", [zd], [ed], lambda en: en.activation(out=et[0:8, 0:64], in_=zt[0:8, 0:64], func=AF.Exp))
            kb.op("act", [ed], [spd], lambda en: en.activation(out=spt[0:8, 0:64], in_=et[0:8, 0:64], func=AF.Ln, bias=C["one"][0:8, :]))
            kb.op("dve", [spd], [spd], lambda en: en.tensor_tensor(out=spt[0:8, 0:64], in0=spt[0:8, 0:64], in1=C["mlt8_bf"], op=ALU.mult))
            kb.op("pe", [spd], [ps.d[BT[0]]],
                  lambda en: en.matmul(ps.t[BT[0]][0:8, 0:64], lhsT=C["tri_bf"][0:8, 0:8], rhs=spt[0:8, 0:64], start=True, stop=True))
            kb.op("pe", [spd], [ps.d[BC[0]]],
                  lambda en: en.matmul(ps.t[BC[0]][:, 0:64], lhsT=C["ones_bf"][0:8, :], rhs=spt[0:8, 0:64], start=True, stop=True))
            kb.op("dve", [zd, spd], [zd], lambda en: en.tensor_tensor(out=zt[0:8, 0:64], in0=zt[0:8, 0:64], in1=spt[0:8, 0:64], op=ALU.subtract))
            kb.op("dve", [zd, ps.d[BT[0]]], [zd],
                  lambda en: en.tensor_tensor(out=zt[0:8, 0:64], in0=zt[0:8, 0:64], in1=ps.t[BT[0]][0:8, 0:64], op=ALU.subtract))
            kb.op("dve", [zd], [zd], lambda en: en.tensor_tensor(out=zt[0:8, 0:64], in0=zt[0:8, 0:64], in1=C["negm8_f"], op=ALU.add))
            kb.op("dve", [ps.d[BC[0]]], [Rd], lambda en: en.tensor_copy(out=Rt[:], in_=ps.t[BC[0]][:, 0:64]))
            kb.op("act", [zd], [wd], lambda en: en.activation(out=wt[0:8, 0:64], in_=zt[0:8, 0:64], func=AF.Exp))
            kb.op("pe", [wd, vnb_d], [ps.d[BA]],
                  lambda en: en.matmul(ps.t[BA][0:64, :], lhsT=wt[0:8, 0:64], rhs=vnb[:], start=True, stop=(npage == 0)))
            pages = list(range(npage - 1, -1, -1))
            batches = [pages[i:i + 8] for i in range(0, npage, 8)]
            for bidx, batch in enumerate(batches):
                nb = len(batch)
                W = nb * 64
                j = bi % 2
                bi += 1
                bS = BS[j]
                zt, zd = z_sb[j]
                et, ed = e_sb[j]
                spt, spd = sp_bf[j]
                wt, wd = w_bf[j]
                vbt, vbd = vbf[j]
                for i, pg in enumerate(batch):
                    kp, kpd = kpg[pgi % 3]
                    vp, vpd = vpg[pgi % 3]
                    kt_, ktd = ktb[pgi % 3]
                    bK = BK[pgi % 2]
                    pgi += 1
                    col = s * npage + pg
                    kb.idma(kp[:], ck_l, idx[:, col:col + 1], [cache_d, idx_d], [kpd])
                    kb.idma(vp[:], cv_l, idx[:, col:col + 1], [cache_d, idx_d], [vpd])
                    for c in range(4):
                        kb.op("pe", [kpd], [ps.d[bK]],
                              lambda en: en.transpose(out=ps.t[bK][:, c * 128:(c + 1) * 128], in_=kp[:, c * 128:(c + 1) * 128],
                                                      identity=C["ident_f"]))
                    _copy(kb, "act", kt_[:], ps.t[bK][:], [ps.d[bK]], [ktd])
                    kb.op("pool", [vpd], [vbd], lambda en: en.tensor_copy(out=vbt[:, i, :], in_=vp[:]))
                    for c in range(4):
                        kb.op("pe", [ktd, qb_d], [ps.d[bS]],
                              lambda en: en.matmul(ps.t[bS][:, i * 64 + c * 16:i * 64 + (c + 1) * 16], lhsT=kt_[:, c * 128:(c + 1) * 128],
                                                   rhs=qblk[:, c, :], start=True, stop=True))
                kb.op("dve", [ps.d[bS], br_d], [zd],
                      lambda en: en.scalar_tensor_tensor(out=zt[:, 0:W], in0=ps.t[bS][:, 0:W], scalar=0.125, in1=biasrow[:, 0:W],
                                                         op0=ALU.mult, op1=ALU.add))
                kb.op("act", [zd], [ed], lambda en: en.activation(out=et[:, 0:W], in_=zt[:, 0:W], func=AF.Exp))
                kb.op("act", [ed], [spd], lambda en: en.activation(out=spt[:, 0:W], in_=et[:, 0:W], func=AF.Ln, bias=C["one"]))
                kb.op("pe", [spd], [ps.d[BT[0]]],
                      lambda en: en.matmul(ps.t[BT[0]][:, 0:W], lhsT=C["tri_bf"], rhs=spt[:, 0:W], start=True, stop=True))
                kb.op("pe", [spd], [ps.d[BC[0]]],
                      lambda en: en.matmul(ps.t[BC[0]][:, 0:W], lhsT=C["ones_bf"], rhs=spt[:, 0:W], start=True, stop=True))
                kb.op("dve", [zd, spd], [zd], lambda en: en.tensor_tensor(out=zt[:, 0:W], in0=zt[:, 0:W], in1=spt[:, 0:W], op=ALU.subtract))
                kb.op("dve", [zd, ps.d[BT[0]]], [zd],
                      lambda en: en.tensor_tensor(out=zt[:, 0:W], in0=zt[:, 0:W], in1=ps.t[BT[0]][:, 0:W], op=ALU.subtract))
                for i in range(nb):
                    sl = slice(i * 64, (i + 1) * 64)
                    kb.op("dve", [zd, Rd], [zd], lambda en: en.tensor_tensor(out=zt[:, sl], in0=zt[:, sl], in1=Rt[:], op=ALU.subtract))
                    if not (bidx == len(batches) - 1 and i == nb - 1):
                        kb.op("dve", [ps.d[BC[0]], Rd], [Rd],
                              lambda en: en.tensor_tensor(out=Rt[:], in0=Rt[:], in1=ps.t[BC[0]][:, sl], op=ALU.add))
                kb.op("act", [zd], [wd], lambda en: en.activation(out=wt[:, 0:W], in_=zt[:, 0:W], func=AF.Exp))
                for i in range(nb):
                    last = (bidx == len(batches) - 1 and i == nb - 1)
                    kb.op("pe", [wd, vbd], [ps.d[BA]],
                          lambda en: en.matmul(ps.t[BA][0:64, :], lhsT=wt[:, i * 64:(i + 1) * 64], rhs=vbt[:, i, :], start=False, stop=last))
            kb.op("act", [ps.d[BA]], [a_d], lambda en: en.copy(out=a_sb[:], in_=ps.t[BA][0:64, :]))
            kb.op("dve", [a_d], [a_d], lambda en: en.tensor_tensor(out=a_sb[:], in0=a_sb[:], in1=C["bmask_f"], op=ALU.mult))
            kb.op("dve", [a_d], [a2_d],
                  lambda en: en.tensor_reduce(out=a2[:], in_=a_sb[:].rearrange("p (h d) -> p d h", d=64), axis=AX.X, op=ALU.add))
            kb.op("dve", [a2_d], [sq_d], lambda en: en.tensor_tensor(out=sq[:], in0=a2[:], in1=a2[:], op=ALU.mult))
            kb.op("dve", [sq_d], [st_d], lambda en: en.tensor_reduce(out=st[:, 0:1], in_=sq[:], axis=AX.X, op=ALU.add))
            kb.op("act", [st_d], [st_d],
                  lambda en: en.activation(out=st[:, 1:2], in_=st[:, 0:1], func=AF.Sqrt, scale=1.0 / 64, bias=C["eps"][0:64, :]))
            kb.op("dve", [st_d], [st_d], lambda en: en.reciprocal(out=st[:, 2:3], in_=st[:, 1:2]))
            for rep in range(2):
                kb.op("dve", [a2_d, st_d, g_hq_d], [an_d],
                      lambda en: en.scalar_tensor_tensor(out=an[:, rep * 64:(rep + 1) * 64], in0=a2[:], scalar=st[:, 2:3], in1=g_hq[:],
                                                         op0=ALU.mult, op1=ALU.mult))
            pbx = ps.t[BX][:].bitcast(BF16)
            kb.op("pe", [an_d], [ps.d[BX]],
                  lambda en: en.transpose(out=pbx[:, 0:64], in_=an[:], identity=C["ident_bf"][0:64, 0:64]))
            tv = pbx[:, 0:64].rearrange("p (c hh q) -> p c hh q", hh=2, q=8)
            kb.op("act", [ps.d[BX]], [mixT_d], lambda en: en.copy(out=mixT[0:64, 0:4, c0:c0 + 8], in_=tv[0:64, :, 0, :]))
            kb.op("act", [ps.d[BX]], [mixT_d], lambda en: en.copy(out=mixT[64:128, 0:4, c0:c0 + 8], in_=tv[64:128, :, 1, :]))


def build_page_idx(kb, C, pt_ap, pt_dep, nseq, npage, npool):
    n = nseq * npage
    pti, pti_d = kb.sb([128, n], I32, "pti")
    ptf, ptf_d = kb.sb([128, n], F32, "ptf")
    idx, idx_d = kb.sb([128, DEPTH, n], I32, "pidx")
    kb.dma("sp", pti[:], pt_ap.rearrange("s p -> (s p)").partition_broadcast(128), [pt_dep], [pti_d])
    kb.op("dve", [pti_d], [ptf_d], lambda en: en.tensor_copy(out=ptf[:], in_=pti[:]))
    kb.op("dve", [ptf_d], [ptf_d],
          lambda en: en.tensor_scalar(out=ptf[:], in0=ptf[:], scalar1=128.0, scalar2=C["iota"], op0=ALU.mult, op1=ALU.add))
    for l in range(DEPTH):
        kb.op("dve", [ptf_d], [idx_d],
              lambda en: en.tensor_scalar(out=idx[:, l, :], in0=ptf[:], scalar1=float(l * npool * 128), scalar2=None, op0=ALU.add))
    return idx, idx_d


def phase_conf(kb, ps, C, gluT, gluT_d, cw, cb, lg, lb, prm_d, cprev, cprev_d, mixT, mixT_d, conf_out, conf_out_d):
    KW = CONF_W
    with kb.scope():
        convT = [kb.sb([128, T], F32, "cconv") for _ in range(4)]
        upad, up_d = kb.sb([128, 30 + P_TOK], F32, "cupad")
        upS, ups_d = kb.sb([128, S_SEQ, 38], F32, "cupads")
        at, at_d = kb.sb([128, T], F32, "ca")
        gt, gt_d = kb.sb([128, T], F32, "cg")
        ctail, ct_d = kb.sb([32, 5, 512], F32, "ctail")
        kb.op("pool", [], [up_d], lambda en: en.memset(upad[:, 0:30], 0.0))
        for cc in range(4):
            cv, cvd = convT[cc]
            kb.dma("sp", at[:], gluT[cc * 128:(cc + 1) * 128, :], [gluT_d], [at_d])
            kb.dma("sp", gt[:], gluT[512 + cc * 128:512 + (cc + 1) * 128, :], [gluT_d], [gt_d])
            kb.dma("sp", upS[:, :, 0:30], cprev[:, cc, :, :], [cprev_d], [ups_d])
            kb.op("act", [gt_d], [gt_d], lambda en: en.activation(out=gt[:], in_=gt[:], func=AF.Sigmoid))
            kb.op("dve", [at_d, gt_d], [up_d],
                  lambda en: en.tensor_tensor(out=upad[:, 30:30 + P_TOK], in0=at[:, 0:P_TOK], in1=gt[:, 0:P_TOK], op=ALU.mult))
            kb.op("dve", [at_d, gt_d], [ups_d],
                  lambda en: en.tensor_tensor(out=upS[:, :, 30:38], in0=at[:, P_TOK:T].rearrange("p (s w) -> p s w", w=8),
                                              in1=gt[:, P_TOK:T].rearrange("p (s w) -> p s w", w=8), op=ALU.mult))
            cs_view = cv[:, P_TOK:T].rearrange("p (s w) -> p s w", w=8)
            kb.op("dve", [up_d, prm_d], [cvd],
                  lambda en: en.tensor_scalar(out=cv[:, 0:P_TOK], in0=upad[:, 0:P_TOK], scalar1=cw[:, cc, 0:1], scalar2=cb[:, cc:cc + 1],
                                              op0=ALU.mult, op1=ALU.add))
            kb.op("dve", [ups_d, prm_d], [cvd],
                  lambda en: en.tensor_scalar(out=cs_view, in0=upS[:, :, 0:8], scalar1=cw[:, cc, 0:1], scalar2=cb[:, cc:cc + 1],
                                              op0=ALU.mult, op1=ALU.add))
            for k in range(1, KW):
                kb.op("dve", [up_d, prm_d, cvd], [cvd],
                      lambda en: en.scalar_tensor_tensor(out=cv[:, 0:P_TOK], in0=upad[:, k:k + P_TOK], scalar=cw[:, cc, k:k + 1],
                                                         in1=cv[:, 0:P_TOK], op0=ALU.mult, op1=ALU.add))
                kb.op("dve", [ups_d, prm_d, cvd], [cvd],
                      lambda en: en.scalar_tensor_tensor(out=cs_view, in0=upS[:, :, k:k + 8], scalar=cw[:, cc, k:k + 1],
                                                         in1=cs_view, op0=ALU.mult, op1=ALU.add))
            b = cc % 2
            kb.op("pe", [up_d], [ps.d[b]],
                  lambda en: en.transpose(out=ps.t[b][0:30, 0:128], in_=upad[:, P_TOK:P_TOK + 30], identity=C["ident_f"]))
            for s in range(3):
                kb.op("pe", [ups_d], [ps.d[b]],
                      lambda en: en.transpose(out=ps.t[b][0:30, (s + 1) * 128:(s + 2) * 128], in_=upS[:, s, 8:38],
                                              identity=C["ident_f"]))
            _copy(kb, "act", ctail[0:30, 0:4, cc * 128:(cc + 1) * 128], ps.t[b][0:30, 0:512].rearrange("p (s c) -> p s c", c=128),
                  [ps.d[b]], [ct_d])
            b2 = 2 + cc % 2
            kb.op("pe", [ups_d], [ps.d[b2]],
                  lambda en: en.transpose(out=ps.t[b2][0:30, 0:128], in_=upS[:, 3, 8:38], identity=C["ident_f"]))
            _copy(kb, "act", ctail[0:30, 4, cc * 128:(cc + 1) * 128], ps.t[b2][0:30, 0:128], [ps.d[b2]], [ct_d])
        kb.store(conf_out.rearrange("s r c -> r s c"), ctail[0:30, :, :], [ct_d], [conf_out_d])
        xc, xc_d = kb.sb([128, 4, 512], F32, "cxc")
        sq, sq_d = kb.sb([128, 4, 512], F32, "csq")
        rs, rs_d = kb.sb([128, 512], F32, "crs")
        un, un_d = kb.sb([128, 512], F32, "cun")
        for bi_, (t0, tn) in enumerate(TOK_BLOCKS):
            bM, bV = 4 + bi_ % 2, 6 + bi_ % 2
            for cc in range(4):
                kb.op("pe", [convT[cc][1]], [ps.d[bM]],
                      lambda en: en.matmul(ps.t[bM][:, 0:tn], lhsT=C["ones_f"], rhs=convT[cc][0][:, t0:t0 + tn], start=(cc == 0), stop=(cc == 3)))
            for cc in range(4):
                kb.op("dve", [ps.d[bM], convT[cc][1]], [xc_d],
                      lambda en: en.scalar_tensor_tensor(out=xc[:, cc, 0:tn], in0=ps.t[bM][:, 0:tn], scalar=-1.0 / 512,
                                                         in1=convT[cc][0][:, t0:t0 + tn], op0=ALU.mult, op1=ALU.add))
            kb.op("act", [xc_d], [sq_d], lambda en: en.activation(out=sq[:, :, 0:tn], in_=xc[:, :, 0:tn], func=AF.Square))
            for cc in range(4):
                kb.op("pe", [sq_d], [ps.d[bV]],
                      lambda en: en.matmul(ps.t[bV][:, 0:tn], lhsT=C["ones_f"], rhs=sq[:, cc, 0:tn], start=(cc == 0), stop=(cc == 3)))
            kb.op("act", [ps.d[bV]], [rs_d],
                  lambda en: en.activation(out=rs[:, 0:tn], in_=ps.t[bV][:, 0:tn], func=AF.Sqrt, scale=1.0 / 512, bias=C["eps"]))
            kb.op("dve", [rs_d], [rs_d], lambda en: en.reciprocal(out=rs[:, 0:tn], in_=rs[:, 0:tn]))
            for cc in range(4):
                kb.op("dve", [xc_d, rs_d], [un_d], lambda en: en.tensor_tensor(out=un[:, 0:tn], in0=xc[:, cc, 0:tn], in1=rs[:, 0:tn], op=ALU.mult))
                kb.op("dve", [un_d, prm_d], [un_d],
                      lambda en: en.tensor_scalar(out=un[:, 0:tn], in0=un[:, 0:tn], scalar1=lg[:, cc:cc + 1], scalar2=lb[:, cc:cc + 1],
                                                  op0=ALU.mult, op1=ALU.add))
                kb.op("act", [un_d], [mixT_d], lambda en: en.activation(out=mixT[:, 12 + cc, t0:t0 + tn], in_=un[:, 0:tn], func=AF.Silu))


def phase_ssd(kb, ps, C, L, scw, scb, sprm_d, dtb, alog, dsk_bc, gn_bc, bc_d, sprev, sprev_d, h0T, h0T_d, mixT, mixT_d,
              ssm_out, ssm_out_d, sconv_out, sconv_out_d):
    xcs, xcs_d = L["xcs"], L["xcs_d"]
    with kb.scope():
        tail, tail_d = kb.sb([128, 12, 5, 3], F32, "stail")
        with kb.scope():
            xp = [kb.sb([128, 3 + P_TOK], F32, "sxp") for _ in range(2)]
            xs_ = [kb.sb([128, S_SEQ, 11], F32, "sxs") for _ in range(2)]
            ac = [kb.sb([128, T], F32, "sac") for _ in range(2)]
            for t_, d_ in xp:
                kb.op("pool", [], [d_], lambda en: en.memset(t_[:, 0:3], 0.0))
            for cc in range(12):
                xpt, xpd = xp[cc % 2]
                xst, xsd = xs_[cc % 2]
                act_, acd = ac[cc % 2]
                kb.dma("sp", xpt[:, 3:3 + P_TOK], L["xbcT"][cc * 128:(cc + 1) * 128, 0:P_TOK], [L["xbcT_d"]], [xpd])
                kb.dma("sp", xst[:, :, 3:11], L["xbcT"][cc * 128:(cc + 1) * 128, P_TOK:T].rearrange("p (s w) -> p s w", w=8),
                       [L["xbcT_d"]], [xsd])
                kb.dma("sp", xst[:, :, 0:3], sprev[:, cc, :, :], [sprev_d], [xsd])
                sv = act_[:, P_TOK:T].rearrange("p (s w) -> p s w", w=8)
                kb.op("dve", [xpd, sprm_d], [acd],
                      lambda en: en.tensor_scalar(out=act_[:, 0:P_TOK], in0=xpt[:, 0:P_TOK], scalar1=scw[:, cc, 0:1], scalar2=scb[:, cc:cc + 1],
                                                  op0=ALU.mult, op1=ALU.add))
                kb.op("dve", [xsd, sprm_d], [acd],
                      lambda en: en.tensor_scalar(out=sv, in0=xst[:, :, 0:8], scalar1=scw[:, cc, 0:1], scalar2=scb[:, cc:cc + 1],
                                                  op0=ALU.mult, op1=ALU.add))
                for k in range(1, 4):
                    kb.op("dve", [xpd, sprm_d, acd], [acd],
                          lambda en: en.scalar_tensor_tensor(out=act_[:, 0:P_TOK], in0=xpt[:, k:k + P_TOK], scalar=scw[:, cc, k:k + 1],
                                                             in1=act_[:, 0:P_TOK], op0=ALU.mult, op1=ALU.add))
                    kb.op("dve", [xsd, sprm_d, acd], [acd],
                          lambda en: en.scalar_tensor_tensor(out=sv, in0=xst[:, :, k:k + 8], scalar=scw[:, cc, k:k + 1], in1=sv,
                                                             op0=ALU.mult, op1=ALU.add))
                kb.op("act", [acd], [acd], lambda en: en.activation(out=act_[:], in_=act_[:], func=AF.Silu))
                kb.store(xcs[cc * 128:(cc + 1) * 128, :], act_[:], [acd], [xcs_d])
                kb.op("pool", [xpd], [tail_d], lambda en: en.tensor_copy(out=tail[:, cc, 0, :], in_=xpt[:, P_TOK:P_TOK + 3]))
                kb.op("pool", [xsd], [tail_d], lambda en: en.tensor_copy(out=tail[:, cc, 1:5, :], in_=xst[:, :, 8:11]))
            ttm, ttm_d = kb.sb([16, CONVD], F32, "stailtm")
            for cc in range(12):
                b = cc % 4
                kb.op("pe", [tail_d], [ps.d[b]],
                      lambda en: en.transpose(out=ps.t[b][0:15, 0:128], in_=tail[:, cc, :, :].rearrange("p s w -> p (s w)"),
                                              identity=C["ident_f"]))
                _copy(kb, "act" if cc % 2 == 0 else "dve", ttm[0:15, cc * 128:(cc + 1) * 128], ps.t[b][0:15, 0:128], [ps.d[b]], [ttm_d])
            kb.store(sconv_out.rearrange("s w c -> (s w) c"), ttm[0:15, :], [ttm_d], [sconv_out_d])

        dtv, dtv_d = kb.sb([16, T], F32, "sdt")
        acs, acs_d = kb.sb([16, T], F32, "sacs")
        msk, msk_d = kb.sb([16, T], F32, "smsk")
        av, av_d = kb.sb([16, T], F32, "sav")
        Ah, Ah_d = kb.sb([16, 2], F32, "sAh")
        sel, sel_d = kb.sb([16, 16, 128], F32, "ssel")
        kb.dma("sp", dtv[:], L["dtT"][:, :], [L["dtT_d"]], [dtv_d])
        kb.op("act", [dtv_d, sprm_d], [dtv_d], lambda en: en.activation(out=dtv[:], in_=dtv[:], func=AF.Exp, bias=dtb[:, 0:1]))
        kb.op("act", [dtv_d], [dtv_d], lambda en: en.activation(out=dtv[:], in_=dtv[:], func=AF.Ln, bias=C["one"][0:16, :]))
        kb.op("act", [sprm_d], [Ah_d], lambda en: en.activation(out=Ah[:, 0:1], in_=alog[:, 0:1], func=AF.Exp))
        kb.op("dve", [Ah_d], [Ah_d], lambda en: en.tensor_scalar(out=Ah[:, 1:2], in0=Ah[:, 0:1], scalar1=-1.0, scalar2=None, op0=ALU.mult))
        kb.op("dve", [dtv_d, Ah_d], [av_d],
              lambda en: en.tensor_scalar(out=av[:], in0=dtv[:], scalar1=Ah[:, 1:2], scalar2=None, op0=ALU.mult))
        kb.op("pool", [], [msk_d], lambda en: en.memset(msk[:], 1.0))
        kb.op("pool", [msk_d], [msk_d],
              lambda en: en.memset(msk[:, 0:P_TOK].rearrange("p (c l) -> p c l", l=128)[:, :, 0:1], 0.0))
        kb.op("pool", [msk_d], [msk_d],
              lambda en: en.memset(msk[:, P_TOK:T].rearrange("p (c l) -> p c l", l=8)[:, :, 0:1], 0.0))
        kb.op("dve", [msk_d, av_d], [acs_d],
              lambda en: en.tensor_tensor_scan(out=acs[:], data0=msk[:], data1=av[:], initial=0.0, op0=ALU.mult, op1=ALU.add))
        for h in range(16):
            kb.op("dve", [], [sel_d],
                  lambda en: en.tensor_scalar(out=sel[:, h, :], in0=C["ones_f"][0:16, :], scalar1=C["ident_f"][0:16, h:h + 1], scalar2=None,
                                              op0=ALU.mult))

        hT, hT_d = kb.sb([128, 1024], F32, "shT")
        hTb, hTb_d = kb.sb([128, 1024], BF16, "shTb")
        smt, smt_d = kb.sb([128, 64], F32, "ssm")
        atb, atb_d = kb.sb([128, 48], F32, "satb")
        dg, dg_d = kb.sb([16, 16], F32, "sdg")
        xsF, xsF_d = kb.sb([128, 12, 128], F32, "sxsF")
        bcb, bcb_d = kb.sb([128, 4, 128], BF16, "sbcb")
        btm, btm_d = kb.sb([128, 2, 128], BF16, "sbtm")
        xtm, xtm_d = kb.sb([128, 1024], F32, "sxtm")
        xf, xf_d = kb.sb([128, 1024], BF16, "sxf")
        xfd, xfd_d = kb.sb([128, 1024], BF16, "sxfd")
        cbm, cbm_d = kb.sb([128, 2, 128], F32, "scbm")
        dsb, dsb_d = kb.sb([128, 4, 128], F32, "sdsb")
        mixb = [kb.sb([128, 4, 128], BF16, "smix") for _ in range(2)]
        ysb, ysb_d = kb.sb([128, 1024], F32, "sy")
        zt, zt_d = kb.sb([128, 1024], F32, "sz")
        ynb, ynb_d = kb.sb([128, 1024], BF16, "syn")
        st, st_d = kb.sb([128, 8], F32, "sst2")
        hout, hout_d = kb.sb([128, 8, 128], F32, "shout")
        chunks = [(i * 128, 128, 0, i == 0, i == 15) for i in range(16)]
        chunks += [(P_TOK + s * 8, 8, s + 1, True, True) for s in range(S_SEQ)]
        for (t0, Lc, seq, first, last) in chunks:
            if first:
                if seq == 0:
                    kb.op("pool", [], [hT_d], lambda en: en.memset(hT[:], 0.0))
                else:
                    kb.dma("sp", hT[:], h0T[seq - 1], [h0T_d], [hT_d])
                kb.op("act", [hT_d], [hTb_d], lambda en: en.copy(out=hTb[:], in_=hT[:]))
            kb.op("pe", [dtv_d], [ps.d[0]], lambda en: en.transpose(out=ps.t[0][0:Lc, 0:16], in_=dtv[:, t0:t0 + Lc], identity=C["ident_f"][0:16, 0:16]))
            kb.op("pe", [acs_d], [ps.d[0]], lambda en: en.transpose(out=ps.t[0][0:Lc, 16:32], in_=acs[:, t0:t0 + Lc], identity=C["ident_f"][0:16, 0:16]))
            kb.op("dve", [ps.d[0]], [smt_d], lambda en: en.tensor_copy(out=smt[0:Lc, 0:32], in_=ps.t[0][0:Lc, 0:32]))
            kb.op("act", [smt_d], [smt_d], lambda en: en.activation(out=smt[0:Lc, 32:48], in_=smt[0:Lc, 16:32], func=AF.Exp))
            kb.op("dve", [acs_d], [dg_d],
                  lambda en: en.tensor_scalar(out=dg[:], in0=C["ident_f"][0:16, 0:16], scalar1=acs[:, t0 + Lc - 1:t0 + Lc], scalar2=None, op0=ALU.mult))
            kb.op("pe", [dg_d], [ps.d[1]], lambda en: en.matmul(ps.t[1][:, 0:16], lhsT=C["ones_f"][0:16, :], rhs=dg[:], start=True, stop=True))
            kb.op("dve", [ps.d[1]], [atb_d], lambda en: en.tensor_copy(out=atb[:, 0:16], in_=ps.t[1][:, 0:16]))
            kb.op("act", [atb_d], [atb_d], lambda en: en.activation(out=atb[:, 16:32], in_=atb[:, 0:16], func=AF.Exp))
            kb.op("dve", [atb_d, smt_d], [atb_d],
                  lambda en: en.tensor_tensor(out=atb[0:Lc, 32:48], in0=atb[0:Lc, 0:16], in1=smt[0:Lc, 16:32], op=ALU.subtract))
            kb.op("act", [atb_d], [atb_d], lambda en: en.activation(out=atb[0:Lc, 32:48], in_=atb[0:Lc, 32:48], func=AF.Exp))
            kb.op("dve", [atb_d, smt_d], [smt_d],
                  lambda en: en.tensor_tensor(out=smt[0:Lc, 48:64], in0=smt[0:Lc, 0:16], in1=atb[0:Lc, 32:48], op=ALU.mult))
            kb.dma("sp", xsF[:, :, 0:Lc], xcs[:, t0:t0 + Lc].rearrange("(c p) t -> p c t", p=128), [xcs_d], [xsF_d])
            kb.op("pool", [xsF_d], [bcb_d], lambda en: en.tensor_copy(out=bcb[:, :, 0:Lc], in_=xsF[:, 8:12, 0:Lc]))
            for half in range(2):
                b = 2 + half
                for c in range(4):
                    cc = half * 4 + c
                    kb.op("pe", [xsF_d], [ps.d[b]],
                          lambda en: en.transpose(out=ps.t[b][0:Lc, c * 128:(c + 1) * 128], in_=xsF[:, cc, 0:Lc], identity=C["ident_f"]))
                kb.op("act", [ps.d[b]], [xtm_d], lambda en: en.copy(out=xtm[0:Lc, half * 512:(half + 1) * 512], in_=ps.t[b][0:Lc, :]))
            pbx = ps.t[4][:].bitcast(BF16)
            for g in range(2):
                kb.op("pe", [bcb_d], [ps.d[4]],
                      lambda en: en.transpose(out=pbx[0:Lc, g * 128:(g + 1) * 128], in_=bcb[:, g, 0:Lc], identity=C["ident_bf"]))
            kb.op("dve", [ps.d[4]], [btm_d],
                  lambda en: en.tensor_copy(out=btm[0:Lc, :, :], in_=pbx[0:Lc, 0:256].rearrange("p (g n) -> p g n", n=128)))
            for h in range(16):
                hs = slice(h * 64, (h + 1) * 64)
                kb.op("dve", [xtm_d, smt_d], [xf_d],
                      lambda en: en.tensor_scalar(out=xf[0:Lc, hs], in0=xtm[0:Lc, hs], scalar1=smt[0:Lc, h:h + 1], scalar2=None, op0=ALU.mult))
                kb.op("pool", [xtm_d, smt_d], [xfd_d],
                      lambda en: en.tensor_scalar(out=xfd[0:Lc, hs], in0=xtm[0:Lc, hs], scalar1=smt[0:Lc, 48 + h:49 + h], scalar2=None, op0=ALU.mult))
            for g in range(2):
                kb.op("pe", [bcb_d], [ps.d[5]],
                      lambda en: en.matmul(ps.t[5][0:Lc, g * 128:g * 128 + Lc], lhsT=bcb[:, g, 0:Lc], rhs=bcb[:, 2 + g, 0:Lc], start=True, stop=True))
            for g in range(2):
                kb.op("dve", [ps.d[5]], [cbm_d],
                      lambda en: en.tensor_tensor(out=cbm[0:Lc, g, 0:Lc], in0=ps.t[5][0:Lc, g * 128:g * 128 + Lc], in1=C["mle_f"][0:Lc, 0:Lc], op=ALU.mult))
            for g in range(2):
                kb.op("pe", [bcb_d, hTb_d], [ps.d[6 + g]],
                      lambda en: en.matmul(ps.t[6 + g][0:Lc, :], lhsT=bcb[:, 2 + g, 0:Lc], rhs=hTb[:, g * 512:(g + 1) * 512], start=True, stop=True))
            for h in range(16):
                hs = slice(h * 64, (h + 1) * 64)
                g = h // 8
                e = h % 8
                kb.op("dve", [ps.d[6 + g], smt_d], [ysb_d],
                      lambda en: en.tensor_scalar(out=ysb[0:Lc, hs], in0=ps.t[6 + g][0:Lc, e * 64:(e + 1) * 64], scalar1=smt[0:Lc, 32 + h:33 + h],
                                                  scalar2=None, op0=ALU.mult))
            for hb in range(4):
                mt, md = mixb[hb % 2]
                g = hb // 2
                bAR = hb % 2
                for i in range(4):
                    h = hb * 4 + i
                    kb.op("pe", [sel_d, acs_d], [ps.d[bAR]],
                          lambda en: en.matmul(ps.t[bAR][0:Lc, i * 128:i * 128 + Lc], lhsT=sel[:, h, 0:Lc], rhs=acs[:, t0:t0 + Lc], start=True, stop=True))
                for i in range(4):
                    h = hb * 4 + i
                    kb.op("dve", [ps.d[bAR], smt_d], [dsb_d],
                          lambda en: en.tensor_scalar(out=dsb[0:Lc, i, 0:Lc], in0=ps.t[bAR][0:Lc, i * 128:i * 128 + Lc], scalar1=smt[0:Lc, 16 + h:17 + h],
                                                      scalar2=0.0, op0=ALU.subtract, op1=ALU.min))
                kb.op("act", [dsb_d], [dsb_d], lambda en: en.activation(out=dsb[0:Lc, :, 0:Lc], in_=dsb[0:Lc, :, 0:Lc], func=AF.Exp))
                for i in range(4):
                    kb.op("dve", [dsb_d, cbm_d], [md],
                          lambda en: en.tensor_tensor(out=mt[0:Lc, i, 0:Lc], in0=dsb[0:Lc, i, 0:Lc], in1=cbm[0:Lc, g, 0:Lc], op=ALU.mult))
                bY = 2 + g
                for i in range(4):
                    h = hb * 4 + i
                    e = h % 8
                    kb.op("pe", [md, xf_d], [ps.d[bY]],
                          lambda en: en.matmul(ps.t[bY][0:Lc, e * 64:(e + 1) * 64], lhsT=mt[0:Lc, i, 0:Lc], rhs=xf[0:Lc, h * 64:(h + 1) * 64], start=True, stop=True))
            for g in range(2):
                kb.op("dve", [ps.d[2 + g], ysb_d], [ysb_d],
                      lambda en: en.tensor_tensor(out=ysb[0:Lc, g * 512:(g + 1) * 512], in0=ysb[0:Lc, g * 512:(g + 1) * 512], in1=ps.t[2 + g][0:Lc, :], op=ALU.add))
            for h in range(16):
                hs = slice(h * 64, (h + 1) * 64)
                kb.op("dve", [xtm_d, bc_d, ysb_d], [ysb_d],
                      lambda en: en.scalar_tensor_tensor(out=ysb[0:Lc, hs], in0=xtm[0:Lc, hs], scalar=dsk_bc[0:Lc, h:h + 1], in1=ysb[0:Lc, hs],
                                                         op0=ALU.mult, op1=ALU.add))
            kb.dma("sp", zt[0:Lc, :], L["z"][t0:t0 + Lc, :], [L["z_d"]], [zt_d])
            kb.op("act", [zt_d], [zt_d], lambda en: en.activation(out=zt[0:Lc, :], in_=zt[0:Lc, :], func=AF.Silu))
            kb.op("dve", [zt_d, ysb_d], [ysb_d], lambda en: en.tensor_tensor(out=ysb[0:Lc, :], in0=ysb[0:Lc, :], in1=zt[0:Lc, :], op=ALU.mult))
            for g in range(2):
                kb.op("act", [ysb_d], [zt_d, st_d],
                      lambda en: en.activation(out=zt[0:Lc, g * 512:(g + 1) * 512], in_=ysb[0:Lc, g * 512:(g + 1) * 512], func=AF.Square,
                                               accum_out=st[0:Lc, g:g + 1]))
            kb.op("act", [st_d], [st_d],
                  lambda en: en.activation(out=st[0:Lc, 2:4], in_=st[0:Lc, 0:2], func=AF.Sqrt, scale=1.0 / 512, bias=C["eps"][0:Lc, :]))
            kb.op("dve", [st_d], [st_d], lambda en: en.reciprocal(out=st[0:Lc, 4:6], in_=st[0:Lc, 2:4]))
            for g in range(2):
                gs = slice(g * 512, (g + 1) * 512)
                kb.op("dve", [ysb_d, st_d, bc_d], [ynb_d],
                      lambda en: en.scalar_tensor_tensor(out=ynb[0:Lc, gs], in0=ysb[0:Lc, gs], scalar=st[0:Lc, 4 + g:5 + g], in1=gn_bc[0:Lc, gs],
                                                         op0=ALU.mult, op1=ALU.mult))
            pbx2 = ps.t[5][:].bitcast(BF16)
            for c in range(8):
                kb.op("pe", [ynb_d], [ps.d[5]],
                      lambda en: en.transpose(out=pbx2[:, c * 128:c * 128 + Lc], in_=ynb[0:Lc, c * 128:(c + 1) * 128], identity=C["ident_bf"][0:Lc, 0:Lc]))
            kb.op("act", [ps.d[5]], [mixT_d],
                  lambda en: en.copy(out=mixT[:, 4:12, t0:t0 + Lc], in_=pbx2.rearrange("p (c t) -> p c t", t=128)[:, :, 0:Lc]))
            for g in range(2):
                kb.op("pe", [btm_d, xfd_d], [ps.d[6 + g]],
                      lambda en: en.matmul(ps.t[6 + g][:, :], lhsT=btm[0:Lc, g, :], rhs=xfd[0:Lc, g * 512:(g + 1) * 512], start=True, stop=True))
            for h in range(16):
                hs = slice(h * 64, (h + 1) * 64)
                g = h // 8
                e = h % 8
                kb.op("dve", [hT_d, atb_d, ps.d[6 + g]], [hT_d],
                      lambda en: en.scalar_tensor_tensor(out=hT[:, hs], in0=hT[:, hs], scalar=atb[:, 16 + h:17 + h], in1=ps.t[6 + g][:, e * 64:(e + 1) * 64],
                                                         op0=ALU.mult, op1=ALU.add))
            kb.op("act", [hT_d], [hTb_d], lambda en: en.copy(out=hTb[:], in_=hT[:]))
            if last:
                for half in range(2):
                    b = half
                    for c in range(4):
                        cc = half * 4 + c
                        kb.op("pe", [hT_d], [ps.d[b]],
                              lambda en: en.transpose(out=ps.t[b][:, c * 128:(c + 1) * 128], in_=hT[:, cc * 128:(cc + 1) * 128], identity=C["ident_f"]))
                    kb.op("act", [ps.d[b]], [hout_d],
                          lambda en: en.copy(out=hout[:, half * 4:(half + 1) * 4, :], in_=ps.t[b][:, :].rearrange("p (c n) -> p c n", n=128)))
                kb.store(ssm_out[seq].rearrange("h p n -> (h p) n").rearrange("(c q) n -> q c n", q=128), hout[:], [hout_d], [ssm_out_d])
```
